# Optimizing a Trainium2 kernel written in Bass

```python
import math
import jax, jax.numpy as jnp
from jax import lax
import numpy as np

D_MODEL = 1024
BATCH = 16
SEQ = 2048
DEPTH = 1
DEC_BATCH = 32
DEC_SEQ = 64
PAST_LEN = 2048

CHUNK = 64
N_HEADS = 8
HEAD_DIM = D_MODEL // N_HEADS
DK = HEAD_DIM // 2
SCALE = DK ** -0.5
N_BUCKETS = 32
MAX_DIST = 128
Q_BLOCK = 128
GM_CHUNK = 128
GM_GROUPS = 8
GM_WIDTH = D_MODEL
GM_GROUP_DIM = GM_WIDTH // GM_GROUPS
D_FF = ((8 * D_MODEL // 3 + 255) // 256) * 256
Q_W = N_HEADS * 2 * DK
K_W = N_HEADS * 2 * DK
V_W = N_HEADS * HEAD_DIM
IN_SIZES = [Q_W, K_W, V_W, GM_WIDTH, GM_WIDTH, D_MODEL, D_MODEL]
IN_W = sum(IN_SIZES)
SPLITS = [int(s) for s in np.cumsum(IN_SIZES)[:-1]]
EPS = 1e-6
NEG = -1e30

kernel_name = "hybrid_diffattn_gmlp_stream_step"


def rms_norm(x, g):
    xf = x.astype(jnp.float32)
    y = xf * lax.rsqrt(jnp.mean(xf * xf, axis=-1, keepdims=True) + EPS)
    return (y * g.astype(jnp.float32)).astype(x.dtype)


def rel_bucket(rel):
    half = N_BUCKETS // 2
    max_exact = half // 2
    ret = jnp.where(rel > 0, half, 0)
    n = jnp.abs(rel)
    nf = jnp.maximum(n, 1).astype(jnp.float32)
    large = max_exact + (jnp.log(nf / max_exact) / math.log(MAX_DIST / max_exact)
                         * (half - max_exact)).astype(jnp.int32)
    large = jnp.minimum(large, half - 1)
    return ret + jnp.where(n < max_exact, n, large)


def rel_bias(table, q_pos, k_pos):
    bucket = rel_bucket(k_pos[None, :] - q_pos[:, None])
    return jnp.moveaxis(table[bucket].astype(jnp.float32), -1, 0)


def project_inputs(x, norm1_g, w_in, q_norm_g, k_norm_g, gm_norm_g):
    B, T, _ = x.shape
    h = rms_norm(x, norm1_g)
    q, k, v, gu, gv, ga, gg = jnp.split(h @ w_in, SPLITS, axis=-1)
    q = rms_norm(q.reshape(B, T, N_HEADS, 2, DK), q_norm_g)
    k = rms_norm(k.reshape(B, T, N_HEADS, 2, DK), k_norm_g)
    v = v.reshape(B, T, N_HEADS, HEAD_DIM)
    gu = jax.nn.gelu(gu)
    gv = rms_norm(jax.nn.gelu(gv), gm_norm_g)
    return q, k, v, gu, gv, ga, gg


def diff_lambda(lq1, lk1, lq2, lk2, lam_init):
    f = lambda a, b: jnp.exp(jnp.sum(a.astype(jnp.float32) * b.astype(jnp.float32)))
    return f(lq1, lk1) - f(lq2, lk2) + lam_init


def diff_attend(q, k, v, q_pos, k_pos, rel_table, lam, subln_g, lam_init):
    B, Tq = q.shape[:2]
    mask = (k_pos[None, :] // CHUNK) <= (q_pos[:, None] // CHUNK)
    bias = rel_bias(rel_table, q_pos, k_pos)
    s = jnp.einsum('bqhmd,bkhmd->bhmqk', q, k).astype(jnp.float32) * SCALE + bias[None, :, None]
    p = jax.nn.softmax(jnp.where(mask, s, NEG), axis=-1)
    a = p[:, :, 0] - lam * p[:, :, 1]
    o = jnp.einsum('bhqk,bkhd->bqhd', a.astype(v.dtype), v)
    o = rms_norm(o, subln_g) * (1.0 - lam_init)
    return o.reshape(B, Tq, N_HEADS * HEAD_DIM)


def prompt_attention(q, k, v, rel_table, lam, subln_g, lam_init):
    B, S = q.shape[:2]
    nb = S // Q_BLOCK
    qb = q.reshape(B, nb, Q_BLOCK, N_HEADS, 2, DK).swapaxes(0, 1)
    k_pos = jnp.arange(S)

    def one(args):
        q_blk, i = args
        q_pos = i * Q_BLOCK + jnp.arange(Q_BLOCK)
        return diff_attend(q_blk, k, v, q_pos, k_pos, rel_table, lam, subln_g, lam_init)

    o = lax.map(one, (qb, jnp.arange(nb)))
    return o.swapaxes(0, 1).reshape(B, S, V_W)


def gmlp_prompt(gu, gv, w_s, b):
    B, S, _ = gv.shape
    nc = S // GM_CHUNK
    vv = gv.reshape(B, nc, GM_CHUNK, GM_GROUPS, GM_GROUP_DIM)
    wm = (w_s * jnp.tril(jnp.ones((GM_CHUNK, GM_CHUNK), w_s.dtype))).astype(vv.dtype)
    z = jnp.einsum('gts,bnsgc->bntgc', wm, vv) + b.T[None, None, :, :, None].astype(vv.dtype)
    return gu * z.reshape(B, S, GM_WIDTH)


def gmlp_sample(gu, gv, w_s, b):
    B, T, _ = gv.shape
    wm = (w_s * jnp.tril(jnp.ones((GM_CHUNK, GM_CHUNK), w_s.dtype)))[:, :T, :T].astype(gv.dtype)
    z = jnp.einsum('gts,bsgc->btgc', wm, gv.reshape(B, T, GM_GROUPS, GM_GROUP_DIM))
    z = z + b[:, :T].T[None, :, :, None].astype(gv.dtype)
    return gu * z.reshape(B, T, GM_WIDTH)


def merge(attn, gm, ga, gg, w_ab, w_gb, w_out):
    y = jax.nn.sigmoid(ga) * (attn @ w_ab) + jax.nn.sigmoid(gg) * (gm @ w_gb)
    return y @ w_out


def swiglu_ffn(x, norm2_g, w_ffn_in, w_ffn_out):
    g, u = jnp.split(rms_norm(x, norm2_g) @ w_ffn_in, 2, axis=-1)
    return (jax.nn.silu(g) * u) @ w_ffn_out


def setup_inputs(seed: int = 0) -> dict:
    key = jax.random.key(seed)
    ks = jax.random.split(key, 24)
    nrm = lambda k, shape, s: jax.random.normal(k, shape, jnp.float32) * s
    gain = lambda k, shape: 1.0 + 0.02 * jax.random.normal(k, shape, jnp.float32)
    return {
        "x_prompt": nrm(ks[0], (BATCH, SEQ, D_MODEL), 1.0),
        "x_sample": nrm(ks[1], (DEC_BATCH, DEC_SEQ, D_MODEL), 1.0),
        "cache_k": nrm(ks[2], (DEPTH, DEC_BATCH, PAST_LEN, N_HEADS, 2, DK), 1.0),
        "cache_v": nrm(ks[3], (DEPTH, DEC_BATCH, PAST_LEN, N_HEADS, HEAD_DIM), 1.0),
        "rel_table": nrm(ks[4], (N_BUCKETS, N_HEADS), 0.5),
        "norm1_g": gain(ks[5], (DEPTH, D_MODEL)),
        "w_in": nrm(ks[6], (DEPTH, D_MODEL, IN_W), D_MODEL ** -0.5),
        "q_norm_g": gain(ks[7], (DEPTH, DK)),
        "k_norm_g": gain(ks[8], (DEPTH, DK)),
        "lambda_q1": nrm(ks[9], (DEPTH, DK), 0.1),
        "lambda_k1": nrm(ks[10], (DEPTH, DK), 0.1),
        "lambda_q2": nrm(ks[11], (DEPTH, DK), 0.1),
        "lambda_k2": nrm(ks[12], (DEPTH, DK), 0.1),
        "subln_g": gain(ks[13], (DEPTH, HEAD_DIM)),
        "gm_norm_g": gain(ks[14], (DEPTH, GM_WIDTH)),
        "gm_w_s": nrm(ks[15], (DEPTH, GM_GROUPS, GM_CHUNK, GM_CHUNK), GM_CHUNK ** -0.5),
        "gm_b": gain(ks[16], (DEPTH, GM_GROUPS, GM_CHUNK)),
        "w_attn_branch": nrm(ks[17], (DEPTH, V_W, D_MODEL), V_W ** -0.5),
        "w_gmlp_branch": nrm(ks[18], (DEPTH, GM_WIDTH, D_MODEL), GM_WIDTH ** -0.5),
        "w_out": nrm(ks[19], (DEPTH, D_MODEL, D_MODEL), D_MODEL ** -0.5),
        "norm2_g": gain(ks[20], (DEPTH, D_MODEL)),
        "w_ffn_in": nrm(ks[21], (DEPTH, D_MODEL, 2 * D_FF), D_MODEL ** -0.5),
        "w_ffn_out": nrm(ks[22], (DEPTH, D_FF, D_MODEL), D_FF ** -0.5),
    }


def reference(x_prompt, x_sample, cache_k, cache_v, rel_table, norm1_g, w_in, q_norm_g,
              k_norm_g, lambda_q1, lambda_k1, lambda_q2, lambda_k2, subln_g, gm_norm_g,
              gm_w_s, gm_b, w_attn_branch, w_gmlp_branch, w_out, norm2_g, w_ffn_in,
              w_ffn_out):
    xp, xs = x_prompt, x_sample
    past = cache_k.shape[2]
    T = xs.shape[1]
    kp_l, vp_l, ks_l, vs_l, gms_l = [], [], [], [], []
    for l in range(DEPTH):
        lam_init = 0.8 - 0.6 * math.exp(-0.3 * l)
        lam = diff_lambda(lambda_q1[l], lambda_k1[l], lambda_q2[l], lambda_k2[l], lam_init)

        q, k, v, gu, gv, ga, gg = project_inputs(xp, norm1_g[l], w_in[l], q_norm_g[l],
                                                 k_norm_g[l], gm_norm_g[l])
        att = prompt_attention(q, k, v, rel_table, lam, subln_g[l], lam_init)
        gm = gmlp_prompt(gu, gv, gm_w_s[l], gm_b[l])
        xp = xp + merge(att, gm, ga, gg, w_attn_branch[l], w_gmlp_branch[l], w_out[l])
        xp = xp + swiglu_ffn(xp, norm2_g[l], w_ffn_in[l], w_ffn_out[l])
        kp_l.append(k)
        vp_l.append(v)

        q, k, v, gu, gv, ga, gg = project_inputs(xs, norm1_g[l], w_in[l], q_norm_g[l],
                                                 k_norm_g[l], gm_norm_g[l])
        k_all = jnp.concatenate([cache_k[l].astype(k.dtype), k], axis=1)
        v_all = jnp.concatenate([cache_v[l].astype(v.dtype), v], axis=1)
        q_pos = past + jnp.arange(T)
        k_pos = jnp.arange(past + T)
        att = diff_attend(q, k_all, v_all, q_pos, k_pos, rel_table, lam, subln_g[l], lam_init)
        gm = gmlp_sample(gu, gv, gm_w_s[l], gm_b[l])
        xs = xs + merge(att, gm, ga, gg, w_attn_branch[l], w_gmlp_branch[l], w_out[l])
        xs = xs + swiglu_ffn(xs, norm2_g[l], w_ffn_in[l], w_ffn_out[l])
        ks_l.append(k)
        vs_l.append(v)
        gms_l.append(gv)

    new_k_prompt = jnp.stack(kp_l)
    new_v_prompt = jnp.stack(vp_l)
    new_k_sample = jnp.stack(ks_l)
    new_v_sample = jnp.stack(vs_l)
    new_gm_v_sample = jnp.stack(gms_l)
    return (xp, xs, new_k_prompt, new_v_prompt, new_k_sample, new_v_sample, new_gm_v_sample)
```

```python
import math
import numpy as np
import concourse.bass as bass
import concourse.mybir as mybir
from concourse.bass_utils import run_bass_kernel_spmd

F32 = mybir.dt.float32
BF16 = mybir.dt.bfloat16
AF = mybir.ActivationFunctionType
ALU = mybir.AluOpType
AX = mybir.AxisListType

D = 1024
H = 8
DK = 64
HD = 128
DFF = 2816
NFC = DFF // 128
INW = 7168
EPS = 1e-6
SCALE = DK ** -0.5
LAM_INIT = 0.8 - 0.6 * math.exp(0.0)
LB = 384
NCORES = 8


def _bucket(rel):
    half = 16
    max_exact = 8
    ret = np.where(rel > 0, half, 0)
    n = np.abs(rel)
    nf = np.maximum(n, 1).astype(np.float32)
    large = max_exact + (np.log(nf / np.float32(max_exact)) / np.float32(math.log(128 / max_exact))
                         * (half - max_exact)).astype(np.int32)
    large = np.minimum(large, half - 1)
    return ret + np.where(n < max_exact, n, large)


class _Stop(Exception):
    pass


STOP = None
_stage = [0]


def stage(name):
    _stage[0] += 1
    if STOP is not None and _stage[0] >= STOP:
        print("STOP at stage", _stage[0], name)
        raise _Stop()


class Sched:
    ENGS = ("pe", "act", "dve", "pool", "sp")

    def __init__(self):
        self.ops = {e: [] for e in self.ENGS}
        self.last_w = {}
        self.readers = {}
        self.dma_cnt = {}
        self.subs = {}

    @staticmethod
    def _base(k):
        if isinstance(k, tuple) and len(k) == 2 and isinstance(k[0], tuple):
            return k[0]
        return None

    def _expand(self, k):
        out = [k]
        b = self._base(k)
        if b is not None:
            self.subs.setdefault(b, set()).add(k)
            out.append(b)
        out += list(self.subs.get(k, ()))
        return out

    def op(self, eng, fn, r=(), w=(), dma=None):
        deps = set()
        for k in r:
            for x in self._expand(k):
                t = self.last_w.get(x)
                if t is not None:
                    deps.add(t)
        for k in w:
            for x in self._expand(k):
                t = self.last_w.get(x)
                if t is not None:
                    deps.add(t)
                for t in self.readers.get(x, ()):
                    deps.add(t)
        idx = len(self.ops[eng])
        if dma is not None:
            prev = self.dma_cnt.get(dma, 0)
            if prev:
                deps.add(("d", dma, prev))
            cnt = prev + 16
            self.dma_cnt[dma] = cnt
            tok = ("d", dma, cnt)
        else:
            tok = ("e", eng, idx)
        if eng == "pe":
            deps = {t for t in deps if not (t[0] == "e" and t[1] == "pe")}
        self.ops[eng].append({"fn": fn, "deps": deps, "dma": dma, "sig": False})
        for k in r:
            self.readers.setdefault(k, []).append(tok)
        for k in w:
            self.last_w[k] = tok
            self.readers[k] = []
            for x in self.subs.get(k, ()):
                self.last_w[x] = tok
                self.readers[x] = []
        return tok

    def fence(self):
        toks = []
        for e in self.ENGS:
            if self.ops[e]:
                toks.append(("e", e, len(self.ops[e]) - 1))
        for name, cnt in self.dma_cnt.items():
            toks.append(("d", name, cnt))
        key = ("fence", len(toks), sum(len(v) for v in self.ops.values()))
        for e in self.ENGS:
            self.ops[e].append({"fn": None, "deps": set(toks), "dma": None, "sig": False})

    def emit(self, nc):
        for e in self.ENGS:
            for o in self.ops[e]:
                for t in o["deps"]:
                    if t[0] == "e":
                        self.ops[t[1]][t[2]]["sig"] = True
        cum = {}
        for e in self.ENGS:
            c = 0
            arr = []
            ops = self.ops[e]
            for i, o in enumerate(ops):
                if o["sig"] and o["fn"] is None:
                    o["sig"] = False
                    j = i - 1
                    while j >= 0 and (ops[j]["fn"] is None or ops[j]["dma"] is not None):
                        j -= 1
                    if j >= 0:
                        ops[j]["sig"] = True
            for o in ops:
                if o["sig"] and o["dma"] is None and o["fn"] is not None:
                    c += 1
                arr.append(c)
            cum[e] = arr
        dma_names = sorted(self.dma_cnt.keys())
        import contextlib
        with contextlib.ExitStack() as st:
            esem = {e: st.enter_context(nc.semaphore("s_" + e)) for e in self.ENGS}
            dsem = {n: st.enter_context(nc.semaphore("d_" + n)) for n in dma_names}
            block = st.enter_context(nc.Block())

            def run(ename, eng):
                waited = {}
                for o in self.ops[ename]:
                    need = {}
                    for t in o["deps"]:
                        if t[0] == "e":
                            val = cum[t[1]][t[2]]
                            if val == 0:
                                continue
                            key = ("e", t[1])
                            need[key] = max(need.get(key, 0), val)
                        else:
                            key = ("d", t[1])
                            need[key] = max(need.get(key, 0), t[2])
                    import os
                    if os.environ.get("DUMP"):
                        print("OP", ename, self.ops[ename].index(o), "needs", {k: v for k, v in need.items() if waited.get(k, 0) < v},
                              "sig" if o["sig"] else "", "dma=%s" % o["dma"] if o["dma"] else "", "nop" if o["fn"] is None else "")
                    for key, val in need.items():
                        if waited.get(key, 0) < val:
                            sem = esem[key[1]] if key[0] == "e" else dsem[key[1]]
                            eng.wait_ge(sem, val)
                            waited[key] = val
                    if o["fn"] is None:
                        continue
                    ins = o["fn"](eng)
                    if o["dma"] is not None:
                        ins.then_inc(dsem[o["dma"]], 16)
                    elif o["sig"]:
                        ins.then_inc(esem[ename], 1)
                if ename == "sp":
                    for n in dma_names:
                        eng.wait_ge(dsem[n], self.dma_cnt[n])

            @block.tensor
            def _(e):
                run("pe", e)

            @block.scalar
            def _(e):
                run("act", e)

            @block.vector
            def _(e):
                run("dve", e)

            @block.gpsimd
            def _(e):
                run("pool", e)

            @block.sync
            def _(e):
                run("sp", e)


class Ring:
    def __init__(self, name, aps):
        self.name = name
        self.aps = aps
        self.i = 0

    def next(self):
        j = self.i % len(self.aps)
        self.i += 1
        return self.aps[j], (self.name, j)


def build(NP, S, NS, PAST):
    assert S % 512 == 0 and PAST % 128 == 0 and NS <= 4
    nc = bass.Bass("TRN2", target_bir_lowering=False)
    sc = Sched()
    NKT = S // 128
    NJ = PAST // 128
    TS = 64

    def din(name, shape):
        return nc.dram_tensor(name, shape, F32, kind="ExternalInput").ap()

    def dout(name, shape):
        return nc.dram_tensor(name, shape, F32, kind="ExternalOutput").ap()

    xp = din("xp", [NP * S, D])
    xs = din("xs", [NS * TS, D])
    ck = din("ck", [NS, PAST, D])
    cv = din("cv", [NS, PAST, D])
    rel_table = din("rel_table", [32, H])
    norm1_g = din("norm1_g", [1, D])
    w_in = din("w_in", [D, INW])
    q_norm_g = din("q_norm_g", [1, DK])
    k_norm_g = din("k_norm_g", [1, DK])
    lq1 = din("lambda_q1", [1, DK])
    lk1 = din("lambda_k1", [1, DK])
    lq2 = din("lambda_q2", [1, DK])
    lk2 = din("lambda_k2", [1, DK])
    subln_g = din("subln_g", [1, HD])
    gm_norm_g = din("gm_norm_g", [1, D])
    gm_w_s = din("gm_w_s", [8, 128, 128])
    gm_b = din("gm_b", [8, 128])
    w_ab = din("w_ab", [D, D])
    w_gb = din("w_gb", [D, D])
    w_out = din("w_out", [D, D])
    norm2_g = din("norm2_g", [1, D])
    w_f1 = din("w_f1", [D, 2 * DFF])
    w_f2 = din("w_f2", [DFF, D])
    c_ident = din("c_ident", [128, 128])
    c_oh = din("c_oh", [32, LB])
    c_tril = din("c_tril", [128, 128])

    yp = dout("yp", [NP * S, D])
    ys = dout("ys", [NS * TS, D])
    kp = dout("kp", [NP * S, D])
    vp = dout("vp", [NP * S, D])
    ksn = dout("ksn", [NS * TS, D])
    vsn = dout("vsn", [NS * TS, D])
    gms = dout("gms", [NS * TS, D])

    wsrc = {"w_in": (w_in, D, INW), "w_ab": (w_ab, D, D), "w_gb": (w_gb, D, D), "w_out": (w_out, D, D),
            "w_f1": (w_f1, D, 2 * DFF), "w_f2": (w_f2, DFF, D)}
    wb = {n: nc.dram_tensor("wb_" + n, [r, c], BF16).ap() for n, (_, r, c) in wsrc.items()}
    rscr = nc.dram_tensor("rscr", [H, 128, LB], F32).ap()

    def sb(name, shape, dt):
        return nc.alloc_sbuf_tensor(name, shape, dt)

    RES_N = max(8 * S + NKT * 8 * 129, 2 * (PAST * 4 + PAST + NJ * 129) + 8 * NS * 64 + NS * 8 * 129 + 64)
    RES = sb("RES", [128, RES_N], BF16)
    KT = RES[:, 0:8 * S].rearrange("p (h k) -> p h k", h=8)
    Vr = RES[:, 8 * S:8 * S + NKT * 8 * 129].rearrange("p (t h c) -> p t h c", t=NKT, h=8)
    units = []
    off = 0
    for u in range(2):
        kst = RES[:, off:off + 2 * PAST].bitcast(F32).rearrange("p (j c) -> p j c", c=128)
        off += 2 * PAST
        vst = RES[:, off:off + 2 * PAST].bitcast(F32).rearrange("p (j c) -> p j c", c=128)
        off += 2 * PAST
        ktu = RES[:, off:off + PAST]
        off += PAST
        vu = RES[:, off:off + NJ * 129].rearrange("p (j c) -> p j c", c=129)
        off += NJ * 129
        off += off % 2
        units.append((kst, vst, ktu, vu))
    KTn = RES[:, off:off + 8 * NS * 64].rearrange("p (h k) -> p h k", h=8)
    off += 8 * NS * 64
    Vn = RES[:, off:off + NS * 8 * 129].rearrange("p (s h c) -> p s h c", s=NS, h=8)
    off += NS * 8 * 129

    MA = sb("MA", [128, 6 * 4096], BF16)
    OFF_M1, OFF_M2, OFF_M5, OFF_M3, OFF_M4, OFF_M0 = 0, 4096, 8192, 12288, 16384, 20480

    def fm(offs):
        return MA[:, offs:offs + 4096].rearrange("p (c n) -> p c n", c=8)

    hT = fm(OFF_M0)
    qT = fm(OFF_M1)
    oT = fm(OFF_M2)
    gmT = qT
    yT = fm(OFF_M5)
    hnT = hT
    gu = MA[:, OFF_M3:OFF_M3 + 4096].rearrange("p (t n) -> p t n", t=4)
    gvb = MA[:, OFF_M4:OFF_M4 + 4096].rearrange("p (t n) -> p t n", t=4)
    x1 = MA[:, OFF_M3:OFF_M3 + 8192].bitcast(F32).rearrange("p (t n) -> p t n", t=4)
    actT = MA[:, 0:NFC * 512].rearrange("p (j n) -> p j n", j=NFC)

    def kfm(offs, t, nr):
        return [("MA", (offs + c * 512 + t * nr) // 128) for c in range(8)]

    def ktm(offs, t):
        return [("MA", (offs + t * 1024) // 128 + i) for i in range(8)]

    def kx1(t):
        return [("MA", (OFF_M3 + t * 2048) // 128 + i) for i in range(16)]

    def kact(j):
        return [("MA", (j * 512) // 128 + i) for i in range(4)]

    NSLOT = 4
    WR = sb("WR", [128, NSLOT * 4096], BF16)
    wring = Ring("wr", [WR[:, i * 4096:(i + 1) * 4096].rearrange("p (k n) -> p k n", k=8) for i in range(NSLOT)])
    NPA, NPB, NPC, NPT, NPS = 8, 3, 4, 6, 24
    PAt = sb("PA", [128, NPA * 512], F32)
    PA = Ring("pa", [PAt[:, i * 512:(i + 1) * 512] for i in range(NPA - 2)])
    PA2 = Ring("pa2", [PAt[:, (NPA - 2) * 512:NPA * 512]])
    PBt = sb("PB", [128, NPB * 1024], F32)
    PB = Ring("pb", [PBt[:, i * 1024:(i + 1) * 1024] for i in range(NPB)])
    PCt = sb("PC", [128, NPC * 1024], BF16)
    PC = Ring("pc", [PCt[:, i * 1024:(i + 1) * 1024] for i in range(NPC)])
    PTt = sb("PT", [128, NPT * 512], BF16)
    PT = Ring("pt", [PTt[:, i * 1024:(i + 1) * 1024] for i in range(NPT // 2)])
    PSt = sb("PS", [128, NPS * 8], F32)
    PS = Ring("ps", [PSt[:, i * 8:(i + 1) * 8] for i in range(NPS)])
    junk = sb("junk", [128, 1024], BF16)

    identf = sb("identf", [128, 128], F32)
    identb = sb("identb", [128, 128], BF16)
    biasT = sb("biasT", [128, H, 256], F32)
    cfar = sb("cfar", [128, H], F32)
    neg_lam = sb("neg_lam", [128, 1], F32)
    gqs = sb("gqs", [128, DK], F32)
    gkb = sb("gkb", [128, DK], F32)
    Gm = sb("Gm", [128, D], F32)
    g1col = sb("g1col", [128, 8], F32)
    g2col = sb("g2col", [128, 8], F32)
    sgcol = sb("sgcol", [128, 1], F32)
    wmT = sb("wmT", [128, 8, 128], BF16)
    bcol = sb("bcol", [128, 8], F32)
    gqcol = sb("gqcol", [128, 1], F32)

    NMM = 6
    MMall = nc.alloc_psum_tensor("mmall", [128, NMM * 512], F32)
    MM = Ring("mm", [MMall[:, i * 512:(i + 1) * 512] for i in range(NMM)])
    ACCall = nc.alloc_psum_tensor("accall", [128, 1024], F32)
    ACCt = [ACCall[:, i * 512:(i + 1) * 512] for i in range(2)]
    class _TR:
        @staticmethod
        def next():
            ap, k = MM.next()
            return ap.bitcast(BF16)[:, 0:512], k
    TR = _TR()
    acc_i = [0]

    deferred = []
    dbg_y = [None]

    def defer(fn, delay=1):
        deferred.append([delay, fn])

    def flush():
        for ent in deferred:
            ent[0] -= 1
        ready = [ent for ent in deferred if ent[0] <= 0]
        for ent in ready:
            deferred.remove(ent)
        for ent in ready:
            ent[1]()

    def flush_all():
        while deferred:
            deferred.pop(0)[1]()

    def rstd_ops(ss, rs, n, width, nr):
        ssa, ssk = ss
        rsa, rsk = rs
        sc.op("act", lambda e: e.activation(out=rsa[:nr, 0:width], in_=ssa[:nr, 0:width], func=AF.Ln,
                                            scale=1.0 / n, bias=EPS), r=[ssk], w=[rsk])
        sc.op("act", lambda e: e.activation(out=rsa[:nr, 0:width], in_=rsa[:nr, 0:width], func=AF.Exp,
                                            scale=-0.5), r=[rsk], w=[rsk])

    evac_flip = [0]

    def evac(out_ap, in_ap, r, w):
        evac_flip[0] ^= 1
        import os
        mode_ = os.environ.get("EVAC", "both")
        if (evac_flip[0] and mode_ == "both") or mode_ == "act":
            sc.op("act", lambda e: e.activation(out=out_ap, in_=in_ap, func=AF.Copy), r=r, w=w)
        else:
            sc.op("dve", lambda e: e.tensor_copy(out=out_ap, in_=in_ap), r=r, w=w)

    def transposes(src, srck, nr, nchunks, dst_fn, dstk, ident=None, scale=None):
        for g0 in range(0, nchunks, 4):
            n = min(4, nchunks - g0)
            tp, tpk = TR.next()

            def f(e, g0=g0, n=n, tp=tp):
                ins = None
                for j in range(n):
                    ins = e.transpose(out=tp[:, j * nr:(j + 1) * nr], in_=src[:nr, (g0 + j) * 128:(g0 + j + 1) * 128],
                                      identity=identb[:nr, :nr])
                return ins
            sc.op("pe", f, r=list(srck) + ["identb"], w=[tpk])
            import os
            tv = os.environ.get("TRV", "3d")
            if scale is not None:
                d__ = dst_fn(g0, n)
                sc.op("act", lambda e, d__=d__, tp=tp, n=n: e.activation(
                    out=d__, in_=tp[:, 0:n * nr].rearrange("p (c n) -> p c n", c=n), func=AF.Copy, scale=scale[:, 0:1]),
                    r=[tpk, "gqcol"], w=dstk)
            elif tv == "3d":
                evac(dst_fn(g0, n), tp[:, 0:n * nr].rearrange("p (c n) -> p c n", c=n), r=[tpk], w=dstk)
            elif tv == "2d":
                d_ = dst_fn(g0, n)
                for j in range(n):
                    evac(d_[:, j, :], tp[:, j * nr:(j + 1) * nr], r=[tpk], w=dstk)

    def load_w(name, rc0, nrc, col0, ncols):
        slot, sk = wring.next()
        src = wb[name][rc0 * 128:(rc0 + nrc) * 128, col0:col0 + ncols].rearrange("(k p) n -> p k n", p=128)
        rk = [("wb", name, rc0 + i, cb) for i in range(nrc) for cb in range(col0 // 1024, (col0 + ncols - 1) // 1024 + 1)]
        sc.op("sp", lambda e: e.dma_start(out=slot[:, 0:nrc, 0:ncols], in_=src), r=rk, w=[sk], dma="wr%d" % sk[1])
        return slot, sk

    def dense_mm(ps, nr, lhs_fn, nk, rhs_fn, ncols, r, psk):
        def f(e):
            ins = None
            for kc in range(nk):
                ins = e.matmul(ps[:nr, 0:ncols], lhsT=lhs_fn(kc), rhs=rhs_fn(kc), start=(kc == 0), stop=(kc == nk - 1))
            return ins
        sc.op("pe", f, r=r, w=[psk])
        flush()

    def setup():
        sc.op("sp", lambda e: e.dma_start(out=identf[:], in_=c_ident[:, :]), w=["identf"], dma="c0")
        sc.op("dve", lambda e: e.tensor_copy(out=identb[:], in_=identf[:]), r=["identf"], w=["identb"])
        sc.op("sp", lambda e: e.dma_start(out=gkb[:], in_=k_norm_g[0:1, :].partition_broadcast(128)), w=["gkb"], dma="c1")
        sc.op("sp", lambda e: e.dma_start(out=gqs[:], in_=q_norm_g[0:1, :].partition_broadcast(128)), w=["gqs"], dma="c2")
        sc.op("dve", lambda e: e.tensor_scalar(out=gqs[:], in0=gqs[:], scalar1=SCALE, scalar2=None, op0=ALU.mult),
              r=["gqs"], w=["gqs"])
        sc.op("sp", lambda e: e.dma_start(out=Gm[:], in_=gm_norm_g[0:1, :].partition_broadcast(128)), w=["Gm"], dma="c3")
        sc.op("sp", lambda e: e.dma_start(out=g1col[:], in_=norm1_g.rearrange("o (c p) -> p (o c)", p=128),
                                          allow_slow_non_contiguous=True), w=["g1col"], dma="c4")
        sc.op("sp", lambda e: e.dma_start(out=g2col[:], in_=norm2_g.rearrange("o (c p) -> p (o c)", p=128),
                                          allow_slow_non_contiguous=True), w=["g2col"], dma="c5")
        sc.op("sp", lambda e: e.dma_start(out=sgcol[:], in_=subln_g.rearrange("o p -> p o"),
                                          allow_slow_non_contiguous=True), w=["sgcol"], dma="c6")
        sc.op("dve", lambda e: e.tensor_scalar(out=sgcol[:], in0=sgcol[:], scalar1=1.0 - LAM_INIT, scalar2=None,
                                               op0=ALU.mult), r=["sgcol"], w=["sgcol"])
        for hf in range(2):
            sc.op("sp", lambda e, hf=hf: e.dma_start(out=gqcol[hf * 64:(hf + 1) * 64, :], in_=q_norm_g.rearrange("o d -> d o"),
                                                    allow_slow_non_contiguous=True), w=[("gqcol_", hf)], dma="c16")
        sc.op("dve", lambda e: e.tensor_scalar(out=gqcol[:], in0=gqcol[:], scalar1=SCALE, scalar2=None, op0=ALU.mult),
              r=[("gqcol_", 0), ("gqcol_", 1)], w=["gqcol"])
        sc.op("sp", lambda e: e.dma_start(out=bcol[:], in_=gm_b.rearrange("g t -> t g"),
                                          allow_slow_non_contiguous=True), w=["bcol"], dma="c7")
        stage('s_loads')
        lt, ltk = PA.next()
        for i, src in enumerate((lq1, lk1, lq2, lk2)):
            sc.op("sp", lambda e, i=i, src=src: e.dma_start(out=lt[:, i * 64:(i + 1) * 64],
                                                            in_=src[0:1, :].partition_broadcast(128)),
                  w=[(ltk, i)], dma="c8")
        st, stk = PS.next()
        pr, prk = PA.next()
        sc.op("dve", lambda e: e.tensor_tensor(out=pr[:, 0:64], in0=lt[:, 0:64], in1=lt[:, 64:128], op=ALU.mult),
              r=[(ltk, 0), (ltk, 1)], w=[(prk, 0)])
        sc.op("dve", lambda e: e.tensor_tensor(out=pr[:, 64:128], in0=lt[:, 128:192], in1=lt[:, 192:256], op=ALU.mult),
              r=[(ltk, 2), (ltk, 3)], w=[(prk, 1)])
        sc.op("dve", lambda e: e.tensor_reduce(out=st[:, 0:2], in_=pr[:, 0:128].rearrange("p (a d) -> p a d", a=2),
                                               axis=AX.X, op=ALU.add), r=[(prk, 0), (prk, 1)], w=[stk])
        sc.op("act", lambda e: e.activation(out=st[:, 2:4], in_=st[:, 0:2], func=AF.Exp), r=[stk], w=[(stk, 1)])
        sc.op("dve", lambda e: e.tensor_tensor(out=neg_lam[:], in0=st[:, 3:4], in1=st[:, 2:3], op=ALU.subtract),
              r=[(stk, 1)], w=["neg_lam"])
        sc.op("dve", lambda e: e.tensor_scalar(out=neg_lam[:], in0=neg_lam[:], scalar1=-LAM_INIT, scalar2=None,
                                               op0=ALU.add), r=["neg_lam"], w=["neg_lam"])
        stage('s_lambda')
        ws, wsk = PB.next()
        ws3 = ws.rearrange("p (g s) -> p g s", g=8)
        tr_, trk = PA.next()
        sc.op("sp", lambda e: e.dma_start(out=ws3, in_=gm_w_s.rearrange("g t s -> t g s")), w=[wsk], dma="c9")
        sc.op("sp", lambda e: e.dma_start(out=tr_[:, 0:128], in_=c_tril[:, :]), w=[trk], dma="c10")
        wmb, wmbk = PC.next()
        sc.op("dve", lambda e: e.tensor_tensor(out=wmb.rearrange("p (g s) -> p g s", g=8), in0=ws3,
                                               in1=tr_[:, 0:128].unsqueeze(1).to_broadcast([128, 8, 128]), op=ALU.mult),
              r=[wsk, trk], w=[wmbk])
        stage('s_gmlp_tt')
        transposes(wmb, [wmbk], 128, 8, lambda c0, n: wmT[:, c0:c0 + n, :], ["wmT"])
        stage('s_gmlp')
        tb, tbk = PA.next()
        oh, ohk = PA.next()
        sc.op("sp", lambda e: e.dma_start(out=tb[0:32, 0:8], in_=rel_table[:, :]), w=[tbk], dma="c11")
        sc.op("sp", lambda e: e.dma_start(out=oh[0:32, 0:LB], in_=c_oh[:, :]), w=[ohk], dma="c12")
        ohb, ohbk = PC.next()
        sc.op("dve", lambda e: e.tensor_copy(out=ohb[0:32, 0:LB], in_=oh[0:32, 0:LB]), r=[ohk], w=[ohbk])
        tB, tBk = PB.next()
        tB3 = tB.rearrange("p (h n) -> p h n", h=8)
        sc.op("dve", lambda e: e.tensor_copy(out=tB3[0:32], in_=tb[0:32, 0:8].unsqueeze(2).to_broadcast([32, 8, 128])),
              r=[tbk], w=[tBk])
        thi, thik = PC.next()
        tlo, tlok = PC.next()
        tres, tresk = PB.next()
        sc.op("dve", lambda e: e.tensor_copy(out=thi[0:32, :], in_=tB[0:32, :]), r=[tBk], w=[thik])
        sc.op("dve", lambda e: e.tensor_copy(out=tres[0:32, :], in_=thi[0:32, :]), r=[thik], w=[tresk])
        sc.op("dve", lambda e: e.tensor_tensor(out=tres[0:32, :], in0=tB[0:32, :], in1=tres[0:32, :], op=ALU.subtract),
              r=[tBk, tresk], w=[tresk])
        sc.op("dve", lambda e: e.tensor_copy(out=tlo[0:32, :], in_=tres[0:32, :]), r=[tresk], w=[tlok])
        for h in range(H):
            ps, psk = MM.next()

            def f(e, h=h, ps=ps):
                e.matmul(ps[:, 0:LB], lhsT=thi[0:32, h * 128:(h + 1) * 128], rhs=ohb[0:32, 0:LB], start=True, stop=False)
                return e.matmul(ps[:, 0:LB], lhsT=tlo[0:32, h * 128:(h + 1) * 128], rhs=ohb[0:32, 0:LB], start=False, stop=True)
            sc.op("pe", f, r=[thik, tlok, ohbk], w=[psk])
            rr, rrk = PA.next()
            sc.op("act", lambda e, ps=ps, rr=rr: e.activation(out=rr[:, 0:LB], in_=ps[:, 0:LB], func=AF.Copy), r=[psk], w=[rrk])
            sc.op("dve", lambda e, h=h, rr=rr: e.tensor_copy(out=cfar[:, h:h + 1], in_=rr[:, LB - 1:LB]), r=[rrk], w=[("cfar", h)])
            sc.op("sp", lambda e, h=h, rr=rr: e.dma_start(out=rscr[h, :, :], in_=rr[:, 0:LB]), r=[rrk], w=[("rscr", h)], dma="c13")
        stage('s_biasmm')
        src_d = bass.AP(tensor=rscr.tensor, offset=127, ap=[[LB - 1, 128], [128 * LB, H], [1, 128]])
        src_s = bass.AP(tensor=rscr.tensor, offset=127 + 128, ap=[[LB - 1, 128], [128 * LB, H], [1, 128]])
        rk = [("rscr", h) for h in range(H)]
        sc.op("sp", lambda e: e.dma_start(out=biasT[:, :, 128:256], in_=src_d), r=rk, w=["biasT_d"], dma="c14")
        sc.op("sp", lambda e: e.dma_start(out=biasT[:, :, 0:128], in_=src_s), r=rk, w=["biasT_s"], dma="c15")
        stage('s_skew')
        sc.op("pool", lambda e: e.memset(biasT[64:128, :, 128:192], -30000.0), r=[], w=["biasT_d"])
        sc.op("pool", lambda e: e.memset(Vr[:, :, :, 128:129], 1.0), w=["Vones"])

    def prepass():
        NST = 4
        stf = [MA[:, i * 2048:(i + 1) * 2048].bitcast(F32) for i in range(NST)]
        stb = [MA[:, NST * 2048 + i * 1024:NST * 2048 + (i + 1) * 1024] for i in range(NST)]
        kf = [[("MA", i * 16 + b) for b in range(16)] for i in range(NST)]
        kb = [[("MA", NST * 16 + i * 8 + b) for b in range(8)] for i in range(NST)]
        cnt = [0]

        def piece(name, rc, cb, ncols, gain):
            src, R, C = wsrc[name]
            i = cnt[0] % NST
            on_act = (cnt[0] % 2 == 0)
            cnt[0] += 1
            f_, b_ = stf[i], stb[i]
            c0 = cb * 1024
            sc.op("sp", lambda e: e.dma_start(out=f_[:, 0:ncols], in_=src[rc * 128:(rc + 1) * 128, c0:c0 + ncols]),
                  w=kf[i], dma="pf%d" % i)
            if gain is None:
                if on_act:
                    sc.op("act", lambda e: e.activation(out=b_[:, 0:ncols], in_=f_[:, 0:ncols], func=AF.Copy), r=kf[i], w=kb[i])
                else:
                    sc.op("dve", lambda e: e.tensor_copy(out=b_[:, 0:ncols], in_=f_[:, 0:ncols]), r=kf[i], w=kb[i])
            else:
                gap, gk = gain
                if on_act:
                    sc.op("act", lambda e: e.activation(out=b_[:, 0:ncols], in_=f_[:, 0:ncols], func=AF.Copy, scale=gap),
                          r=kf[i] + [gk], w=kb[i])
                else:
                    sc.op("dve", lambda e: e.tensor_scalar(out=b_[:, 0:ncols], in0=f_[:, 0:ncols], scalar1=gap, scalar2=None,
                                                           op0=ALU.mult), r=kf[i] + [gk], w=kb[i])
            sc.op("pool", lambda e: e.dma_start(out=wb[name][rc * 128:(rc + 1) * 128, c0:c0 + ncols], in_=b_[:, 0:ncols]),
                  r=kb[i], w=[("wb", name, rc, cb)], dma="pb%d" % i)

        for cb in range(INW // 1024):
            for rc in range(8):
                piece("w_in", rc, cb, 1024, (g1col[:, rc:rc + 1], "g1col"))
        if OVERLAP:
            return
        for name, rc, cb, ncols, gain in part2_list():
            piece(name, rc, cb, ncols, gain)

    def part2_list():
        out = []
        for rc in range(8):
            out.append(("w_ab", rc, 0, 1024, (sgcol[:, 0:1], "sgcol")))
        for rc in range(8):
            out.append(("w_gb", rc, 0, 1024, None))
        for rc in range(8):
            out.append(("w_out", rc, 0, 1024, None))
        ncb = (2 * DFF + 1023) // 1024
        for cb in range(ncb):
            for rc in range(8):
                out.append(("w_f1", rc, cb, min(1024, 2 * DFF - cb * 1024), (g2col[:, rc:rc + 1], "g2col")))
        for rc in range(NFC):
            out.append(("w_f2", rc, 0, 1024, None))
        return out

    OVERLAP = (S >= 2048)
    p2_state = {"gen": None}

    def part2_gen():
        NS2 = 5
        f2 = [KT[:, i, 1024:2048].bitcast(F32) for i in range(NS2)]
        b2 = [KT[:, 5 + i // 2, 1024 + (i % 2) * 512:1024 + (i % 2) * 512 + 512] for i in range(NS2)]
        cnt = 0
        for name, rc, cb, ncols, gain in part2_list():
            src, R, C = wsrc[name]
            for half in range((ncols + 511) // 512):
                c0 = cb * 1024 + half * 512
                nc_ = min(512, cb * 1024 + ncols - c0)
                i = cnt % NS2
                on_act = (cnt % 2 == 0)
                cnt += 1
                f_, b_ = f2[i], b2[i]
                sc.op("sp", lambda e, f_=f_, src=src, rc=rc, c0=c0, nc_=nc_: e.dma_start(
                    out=f_[:, 0:nc_], in_=src[rc * 128:(rc + 1) * 128, c0:c0 + nc_]), w=[("st2f", i)], dma="p2f%d" % i)
                if gain is None:
                    if on_act:
                        sc.op("act", lambda e, f_=f_, b_=b_, nc_=nc_: e.activation(out=b_[:, 0:nc_], in_=f_[:, 0:nc_], func=AF.Copy),
                              r=[("st2f", i)], w=[("st2b", i)])
                    else:
                        sc.op("dve", lambda e, f_=f_, b_=b_, nc_=nc_: e.tensor_copy(out=b_[:, 0:nc_], in_=f_[:, 0:nc_]),
                              r=[("st2f", i)], w=[("st2b", i)])
                else:
                    gap, gk = gain
                    if on_act:
                        sc.op("act", lambda e, f_=f_, b_=b_, nc_=nc_, gap=gap: e.activation(
                            out=b_[:, 0:nc_], in_=f_[:, 0:nc_], func=AF.Copy, scale=gap), r=[("st2f", i), gk], w=[("st2b", i)])
                    else:
                        sc.op("dve", lambda e, f_=f_, b_=b_, nc_=nc_, gap=gap: e.tensor_scalar(
                            out=b_[:, 0:nc_], in0=f_[:, 0:nc_], scalar1=gap, scalar2=None, op0=ALU.mult),
                            r=[("st2f", i), gk], w=[("st2b", i)])
                sc.op("pool", lambda e, b_=b_, name=name, rc=rc, c0=c0, nc_=nc_: e.dma_start(
                    out=wb[name][rc * 128:(rc + 1) * 128, c0:c0 + nc_], in_=b_[:, 0:nc_]),
                    r=[("st2b", i)], w=[(("wb", name, rc, cb), half)], dma="p2b%d" % i)
                yield

    def pump(n):
        g = p2_state["gen"]
        if g is None:
            return
        for _ in range(n):
            try:
                next(g)
            except StopIteration:
                p2_state["gen"] = None
                return

    def p2_release():
        keys = [("st2f", i) for i in range(5)] + [("st2b", i) for i in range(5)]
        for eng in ("act", "dve"):
            sc.op(eng, None, w=keys)

    def attn_head(t, nr, h, ktiles, obuf, obk, ssq_, fillers=None, pv_delay=2):
        qk_ = kfm(OFF_M1, t, nr)
        a0 = ACCt[0][:, 0:129]
        a1 = ACCt[1][:, 0:129]
        acc3 = ACCall[:, :].rearrange("p (m c) -> p m c", m=2)
        ak = ("acc", 0)
        nkt = len(ktiles)
        chunks = [list(range(i, min(i + 4, nkt))) for i in range(0, nkt, 4)]
        for ci, ch in enumerate(chunks):
            if MM.i % 2 == 1:
                MM.next()
            psA, pak = MM.next()
            psB, pbk = MM.next()
            pair = (MM.i - 2) % NMM
            ps2 = MMall[:, pair * 512:pair * 512 + 1024].rearrange("p (m c) -> p m c", m=2)
            rkeys = list(qk_)
            for i in ch:
                rkeys += ktiles[i]["rk"]

            def fqk(e, ch=ch, psA=psA, psB=psB):
                ins = None
                for m, ps in ((0, psA), (1, psB)):
                    for j, i in enumerate(ch):
                        kt = ktiles[i]
                        ins = e.matmul(ps[:kt["nk"], j * nr:(j + 1) * nr], lhsT=kt["kT"](m),
                                       rhs=qT[64 * m:64 * m + 64, h, t * nr:(t + 1) * nr], start=True, stop=True)
                return ins
            sc.op("pe", fqk, r=rkeys, w=[pak, pbk])
            flush()
            pt2_, ptk = PT.next()
            pt2 = pt2_.rearrange("p (m c) -> p m c", m=2)
            ptA, ptB = pt2_[:, 0:512], pt2_[:, 512:1024]
            ptak = ptbk = ptk
            nfar = 0
            while nfar < len(ch) and ktiles[ch[nfar]]["bias"] == "far":
                nfar += 1
            if nfar:
                sc.op("act", lambda e, ps2=ps2, pt2=pt2, nfar=nfar: e.activation(
                    out=pt2[:, :, 0:nfar * nr], in_=ps2[:, :, 0:nfar * nr], func=AF.Exp, bias=cfar[:, h:h + 1]),
                    r=[pak, pbk, ("cfar", h)], w=[(ptk, jj_) for jj_ in range(nfar)])
            j = nfar
            while j < len(ch):
                kt = ktiles[ch[j]]
                nk = kt["nk"]
                bo = 0 if kt["bias"] == "sub" else 128
                w_ = 1
                if (kt["bias"] == "sub" and j + 1 < len(ch) and ktiles[ch[j + 1]]["bias"] == "diag"
                        and ktiles[ch[j + 1]]["nk"] == nk and nr == 128):
                    w_ = 2
                tmp, tmpk = PA.next()
                tmp3 = tmp[:, 0:2 * w_ * nr].rearrange("p (m c) -> p m c", m=2)
                sc.op("dve", lambda e, ps2=ps2, tmp3=tmp3, j=j, nk=nk, bo=bo, w_=w_: e.tensor_tensor(
                    out=tmp3[:nk], in0=ps2[:nk, :, j * nr:(j + w_) * nr],
                    in1=biasT[:nk, h, bo:bo + w_ * nr].unsqueeze(1).to_broadcast([nk, 2, w_ * nr]), op=ALU.add),
                    r=[pak, pbk, "biasT_d", "biasT_s"] + ([(ptk, 0)] if nfar else []), w=[tmpk])
                sc.op("act", lambda e, pt2=pt2, tmp3=tmp3, j=j, nk=nk, w_=w_: e.activation(
                    out=pt2[:nk, :, j * nr:(j + w_) * nr], in_=tmp3[:nk], func=AF.Exp),
                    r=[tmpk], w=[(ptk, jj_) for jj_ in range(j, j + w_)])
                j += w_
            last = (ci == len(chunks) - 1)

            def pv(ch=ch, ptA=ptA, ptB=ptB, ptak=ptak, ptbk=ptbk, nfar=nfar, last=last):
                rk = [(ptak, j) for j in range(len(ch))]
                for i in ch:
                    rk += ktiles[i]["vk"]
                rk.append("Vones")

                def fpv(e):
                    ins = None
                    for m, pt, a in ((0, ptA, a0), (1, ptB, a1)):
                        for j, i in enumerate(ch):
                            kt = ktiles[i]
                            nk = kt["nk"]
                            ins = e.matmul(a[:nr, :], lhsT=pt[:nk, j * nr:(j + 1) * nr], rhs=kt["v"],
                                           start=(i == 0), stop=(i == nkt - 1))
                    return ins
                sc.op("pe", fpv, r=rk, w=[ak])
                if last:
                    st, stk = PS.next()
                    sc.op("dve", lambda e: e.reciprocal(out=st[:nr, 0:2], in_=acc3[:nr, :, 128]), r=[ak], w=[(stk, 0)])
                    sc.op("dve", lambda e: e.tensor_tensor(out=st[:nr, 2:3], in0=st[:nr, 1:2], in1=neg_lam[:nr, :], op=ALU.mult),
                          r=[(stk, 0), "neg_lam"], w=[(stk, 2)])
                    oh_ = obuf[:nr, h * 128:(h + 1) * 128]
                    sc.op("act", lambda e: e.activation(out=oh_, in_=a0[:nr, 0:128], func=AF.Copy, scale=st[:nr, 0:1]),
                          r=[ak, (stk, 0)], w=[(obk, h)])
                    sc.op("dve", lambda e: e.scalar_tensor_tensor(out=oh_, in0=a1[:nr, 0:128], scalar=st[:nr, 2:3], in1=oh_,
                                                                  op0=ALU.mult, op1=ALU.add),
                          r=[ak, (stk, 2), (obk, h)], w=[(obk, h)])
            defer(pv, pv_delay)
            if fillers:
                if "all" in fillers:
                    if ci == 0:
                        for f_ in fillers["all"]:
                            f_()
                else:
                    key_ = ci if ci < len(chunks) - 2 else ci - len(chunks)
                    for f_ in fillers.get(key_, []):
                        f_()

    def attn_finish(t, nr, obuf, obk, ssq_):
        def fin():
            st, stk = ssq_
            rs, rsk = PS.next()
            ok = [(obk, h) for h in range(H)]
            sq, sqk = PA2.next()
            sc.op("act", lambda e: e.activation(out=sq[:nr, :], in_=obuf[:nr, :], func=AF.Square), r=ok, w=[sqk])
            sc.op("dve", lambda e: e.tensor_reduce(out=st[:nr, 0:8], in_=sq[:nr, :].rearrange("p (h d) -> p h d", h=8),
                                                   axis=AX.X, op=ALU.add), r=[sqk], w=[stk])
            import os
            if os.environ.get("DBG") == "obuf" and dbg_y[0] is not None:
                ydst = dbg_y[0][t]["y"]
                sc.op("pool", lambda e: e.dma_start(out=ydst, in_=obuf[:nr, :]), r=ok, w=[], dma="dbg")
            rstd_ops((st, stk), (rs, rsk), HD, 8, nr)
            on, onk = PC.next()
            sc.op("dve", lambda e: e.tensor_tensor(out=on[:nr, :].rearrange("p (h d) -> p h d", h=8),
                                                   in0=obuf[:nr, :].rearrange("p (h d) -> p h d", h=8),
                                                   in1=rs[:nr, 0:8].unsqueeze(2).to_broadcast([nr, 8, 128]), op=ALU.mult),
                  r=ok + [rsk], w=[onk])
            if os.environ.get("DBG") == "on" and dbg_y[0] is not None:
                ydst2 = dbg_y[0][t]["y"]
                sc.op("pool", lambda e: e.dma_start(out=ydst2, in_=on[:nr, :]), r=[onk], w=[], dma="dbg")
            defer(lambda: transposes(on, [onk], nr, 8, lambda c0, n: oT[:, c0:c0 + n, t * nr:(t + 1) * nr], kfm(OFF_M2, t, nr)), 5)
        defer(fin, 2)

    xpre = {}

    def macro(tiles, mode, seq=None, qi0=0, next_tiles=None):
        nt = len(tiles)
        nr = tiles[0]["nr"]
        dbg_y[0] = tiles
        ntok = nt * nr
        ss4, ss4k = PS.next()
        rs4, rs4k = PS.next()
        xts = []
        pre_ = xpre.pop(id(tiles), None)
        for t, tl in enumerate(tiles):
            if pre_ is not None:
                xt, xk = pre_[t]
            else:
                xt, xk = PB.next() if t < 3 else PA2.next()
                sc.op("sp", lambda e, xt=xt, tl=tl: e.dma_start(out=xt[:nr, :], in_=tl["x"]), w=[xk], dma="xld%d" % t)
            xts.append((xt, xk))
            sc.op("act", lambda e, xt=xt, t=t: e.activation(out=junk[:nr, :], in_=xt[:nr, :], func=AF.Square,
                                                            accum_out=ss4[:nr, t:t + 1]), r=[xk], w=[(ss4k, t), "junk"])
        rstd_ops((ss4, ss4k), (rs4, rs4k), D, nt, nr)
        for t, tl in enumerate(tiles):
            xt, xk = xts[t]
            xn, xnk = PC.next()
            sc.op("dve", lambda e, xt=xt, xn=xn, t=t: e.tensor_scalar(out=xn[:nr, :], in0=xt[:nr, :], scalar1=rs4[:nr, t:t + 1],
                                                                      scalar2=None, op0=ALU.mult), r=[xk, rs4k], w=[xnk])
            transposes(xn, [xnk], nr, 8, lambda c0, n, t=t: hT[:, c0:c0 + n, t * nr:(t + 1) * nr], kfm(OFF_M0, t, nr))
        stage('P0')

        def hT_l(t):
            return lambda kc: hT[:, kc, t * nr:(t + 1) * nr]

        def wr(slot, ncols=512, c0=0):
            return lambda kc: slot[:, kc, c0:c0 + ncols]

        import os
        cgs = [int(c) for c in os.environ.get("P1_CGS", "2,0,4,3,1,5").split(",")]
        for cg in cgs:
            slot, sk = load_w("w_in", 0, 8, cg * 512, 512)
            for t, tl in enumerate(tiles):
                ps, psk = MM.next()
                dense_mm(ps, nr, hT_l(t), 8, wr(slot), 512, kfm(OFF_M0, t, nr) + [sk], psk)
                if cg < 4:
                    isq = cg < 2
                    sq, sqk = PA.next()
                    st, stk = PS.next()
                    rs, rsk = PS.next()
                    sc.op("act", lambda e, ps=ps, sq=sq: e.activation(out=sq[:nr, :], in_=ps[:nr, :], func=AF.Square),
                          r=[psk], w=[sqk])
                    sc.op("dve", lambda e, sq=sq, st=st: e.tensor_reduce(
                        out=st[:nr, 0:8], in_=sq[:nr, :].rearrange("p (a d) -> p a d", d=64), axis=AX.X, op=ALU.add),
                        r=[sqk], w=[stk])
                    rstd_ops((st, stk), (rs, rsk), DK, 8, nr)
                    qb, qbk = PC.next()
                    if isq:
                        sc.op("dve", lambda e, ps=ps, qb=qb, rs=rs: e.tensor_tensor(
                            out=qb[:nr, 0:512].rearrange("p (a d) -> p a d", d=64), in0=ps[:nr, :].rearrange("p (a d) -> p a d", d=64),
                            in1=rs[:nr, 0:8].unsqueeze(2).to_broadcast([nr, 8, 64]), op=ALU.mult), r=[psk, rsk], w=[qbk])

                        def dstq(c0, n, t=t, cg=cg):
                            return qT[:, cg * 4 + c0:cg * 4 + c0 + n, t * nr:(t + 1) * nr]
                        defer(lambda qb=qb, qbk=qbk, dstq=dstq, t=t: transposes(
                            qb, [qbk], nr, 4, dstq, kfm(OFF_M1, t, nr), scale=gqcol), 3)
                    else:
                        hh = cg - 2
                        sc.op("dve", lambda e, ps=ps, sq=sq, rs=rs: e.tensor_tensor(
                            out=sq[:nr, :].rearrange("p (a d) -> p a d", d=64), in0=ps[:nr, :].rearrange("p (a d) -> p a d", d=64),
                            in1=rs[:nr, 0:8].unsqueeze(2).to_broadcast([nr, 8, 64]), op=ALU.mult), r=[psk, rsk], w=[sqk])
                        sc.op("pool", lambda e, sq=sq: e.tensor_tensor(
                            out=sq[:nr, :].rearrange("p (a d) -> p a d", d=64), in0=sq[:nr, :].rearrange("p (a d) -> p a d", d=64),
                            in1=gkb[:nr, :].unsqueeze(1).to_broadcast([nr, 8, 64]), op=ALU.mult), r=[sqk, "gkb"], w=[sqk])
                        sc.op("pool", lambda e, sq=sq, tl=tl, hh=hh: e.dma_start(out=tl["k"][:, hh * 512:(hh + 1) * 512], in_=sq[:nr, :]),
                              r=[sqk], w=[], dma="pad%d" % sqk[1])
                        sc.op("act", lambda e, sq=sq, qb=qb: e.activation(out=qb[:nr, 0:512], in_=sq[:nr, :], func=AF.Copy), r=[sqk], w=[qbk])
                        if mode == "prompt":
                            kt_i = qi0 + t

                            def dstk(c0, n, kt_i=kt_i, hh=hh):
                                return KT[:, hh * 4 + c0:hh * 4 + c0 + n, kt_i * 128:(kt_i + 1) * 128]
                            wkeys = [("KT", kt_i, hh)]
                        else:
                            def dstk(c0, n, t=t, hh=hh):
                                return KTn[:, hh * 4 + c0:hh * 4 + c0 + n, t * 64:(t + 1) * 64]
                            wkeys = [("KTn", t, hh)]
                        defer(lambda qb=qb, qbk=qbk, dstk=dstk, wkeys=wkeys: transposes(qb, [qbk], nr, 4, dstk, wkeys), 3)
                else:
                    hh = cg - 4
                    vf, vfk = PA.next()
                    sc.op("act", lambda e, ps=ps, vf=vf: e.activation(out=vf[:nr, :], in_=ps[:nr, :], func=AF.Copy), r=[psk], w=[vfk])
                    sc.op("pool", lambda e, vf=vf, tl=tl, hh=hh: e.dma_start(out=tl["v"][:, hh * 512:(hh + 1) * 512], in_=vf[:nr, :]),
                          r=[vfk], w=[], dma="pad%d" % vfk[1])
                    if mode == "prompt":
                        kt_i = qi0 + t
                        dst = Vr[:, kt_i, hh * 4:hh * 4 + 4, 0:128]
                        wk = [("V", kt_i, hh)]
                    else:
                        dst = Vn[0:64, t, hh * 4:hh * 4 + 4, 0:128]
                        wk = [("Vn", t, hh)]
                    sc.op("dve", lambda e, vf=vf, dst=dst: e.tensor_copy(out=dst[:nr], in_=vf[:nr, :].rearrange("p (h d) -> p h d", h=4)),
                          r=[vfk], w=wk)
        flush_all()
        stage('P1')
        if mode == "prompt":
            for t in range(nt):
                qi = qi0 + t
                obuf, obk = PB.next()
                ssq_ = PS.next()
                for h in range(H):
                    hh = h // 4
                    kts = []
                    for kt_i in range(qi + 1):
                        bias = "far" if kt_i <= qi - 2 else ("sub" if kt_i == qi - 1 else "diag")
                        kts.append({"kT": (lambda m, kt_i=kt_i, h=h: KT[64 * m:64 * m + 64, h, kt_i * 128:(kt_i + 1) * 128]),
                                    "v": Vr[:, kt_i, h, :], "nk": 128, "bias": bias,
                                    "rk": [("KT", kt_i, hh)], "vk": [("V", kt_i, hh)]})
                    attn_head(t, nr, h, kts, obuf, obk, ssq_)
                    pump(4)
                attn_finish(t, nr, obuf, obk, ssq_)
        else:
            def unit_load(s, h):
                ui = (s * H + h) % 2
                kst, vst, ktu, vu = units[ui]
                uk = ("unit", ui)
                srck = ck[s].rearrange("(j p) (h c) -> p j h c", p=128, h=8)[:, :, h, :]
                srcv = cv[s].rearrange("(j p) (h c) -> p j h c", p=128, h=8)[:, :, h, :]
                sc.op("sp", lambda e: e.dma_start(out=kst, in_=srck), w=[(uk, "kst")], dma="uk%d" % ui)
                sc.op("sp", lambda e: e.dma_start(out=vst, in_=srcv), w=[(uk, "vst")], dma="uv%d" % ui)
                sc.op("dve", lambda e: e.tensor_copy(out=vu[:, :, 0:128], in_=vst), r=[(uk, "vst")], w=[(uk, "vu")])

            def unit_tr(s, h):
                ui = (s * H + h) % 2
                kst, vst, ktu, vu = units[ui]
                uk = ("unit", ui)
                outl = []
                for j0 in range(0, NJ, 4):
                    def g(j0=j0):
                        n = min(4, NJ - j0)
                        ps, psk = MM.next()

                        def ftr(e):
                            ins = None
                            for j in range(n):
                                ins = e.transpose(out=ps[:, j * 128:(j + 1) * 128], in_=kst[:, j0 + j, :], identity=identf[:, :])
                            return ins
                        sc.op("pe", ftr, r=[(uk, "kst"), "identf"], w=[psk])
                        evac(ktu[:, j0 * 128:(j0 + n) * 128], ps[:, 0:n * 128], r=[psk], w=[(uk, "ktu", j0 // 4)])
                    outl.append(g)
                return outl

            U = [(s_, h_) for s_ in range(nt) for h_ in range(H)]
            unit_load(*U[0])
            for g in unit_tr(*U[0]):
                g()
            for t in range(nt):
                s = t
                obuf, obk = PB.next()
                ssq_ = PS.next()
                for h in range(H):
                    hh = h // 4
                    ui = (s * H + h) % 2
                    if NJ < 4:
                        flush_all()
                    kst, vst, ktu, vu = units[ui]
                    uk = ("unit", ui)
                    idx = s * H + h
                    fillers = None
                    if idx + 1 < len(U):
                        nx = U[idx + 1]
                        trs = unit_tr(*nx)
                        half_ = (len(trs) + 1) // 2
                        fillers = {0: [lambda nx=nx: unit_load(*nx)], -2: trs[:half_], -1: trs[half_:]}
                        nchunks_ = (NJ + 1 + 3) // 4
                        if nchunks_ < 3:
                            fillers = {"all": [lambda nx=nx: unit_load(*nx)] + trs}
                    kts = []
                    for j in range(NJ):
                        bias = "far" if j < NJ - 1 else "sub"
                        kts.append({"kT": (lambda m, j=j, ktu=ktu: ktu[64 * m:64 * m + 64, j * 128:(j + 1) * 128]),
                                    "v": vu[:, j, :], "nk": 128, "bias": bias,
                                    "rk": [(uk, "ktu", j // 4)], "vk": [(uk, "vu"), ("vuones", ui)]})
                    kts.append({"kT": (lambda m, h=h, s=s: KTn[64 * m:64 * m + 64, h, s * 64:(s + 1) * 64]),
                                "v": Vn[0:64, s, h, :], "nk": 64, "bias": "diag",
                                "rk": [("KTn", s, hh)], "vk": [("Vn", s, hh), "Vnones"]})
                    attn_head(t, nr, h, kts, obuf, obk, ssq_, fillers=fillers, pv_delay=1)
                attn_finish(t, nr, obuf, obk, ssq_)
        def p4_tile(t):
            gmb, gmbk = PC.next()
            for half in range(2):
                ps, psk = MM.next()

                def fz(e, ps=ps, half=half, t=t):
                    ins = None
                    for j in range(4):
                        g = half * 4 + j
                        ins = e.matmul(ps[:nr, j * 128:(j + 1) * 128], lhsT=wmT[:nr, g, :nr], rhs=gvb[:nr, t, g * 128:(g + 1) * 128],
                                       start=True, stop=True)
                    return ins
                sc.op("pe", fz, r=ktm(OFF_M4, t) + ["wmT"], w=[psk])
                flush()
                tmp, tmpk = PA.next()
                sc.op("dve", lambda e, ps=ps, tmp=tmp, half=half: e.tensor_tensor(
                    out=tmp[:nr, :].rearrange("p (g c) -> p g c", g=4), in0=ps[:nr, :].rearrange("p (g c) -> p g c", g=4),
                    in1=bcol[:nr, half * 4:half * 4 + 4].unsqueeze(2).to_broadcast([nr, 4, 128]), op=ALU.add),
                    r=[psk, "bcol"], w=[tmpk])
                sc.op("dve", lambda e, tmp=tmp, gmb=gmb, half=half, t=t: e.tensor_tensor(
                    out=gmb[:nr, half * 512:(half + 1) * 512], in0=tmp[:nr, :], in1=gu[:nr, t, half * 512:(half + 1) * 512], op=ALU.mult),
                    r=[tmpk] + ktm(OFF_M3, t), w=[(gmbk, half)])
            import os
            if os.environ.get("DBG") == "gmb":
                sc.op("pool", lambda e, gmb=gmb, t=t: e.dma_start(out=tiles[t]["y"], in_=gmb[:nr, :]), r=[(gmbk, 0), (gmbk, 1)], w=[], dma="dbg")
            defer(lambda gmb=gmb, gmbk=gmbk, t=t: transposes(
                gmb, [(gmbk, 0), (gmbk, 1)], nr, 8, lambda c0, n: gmT[:, c0:c0 + n, t * nr:(t + 1) * nr], kfm(OFF_M1, t, nr)), 3)

        for cg in (6, 7):
            slot, sk = load_w("w_in", 0, 8, cg * 512, 512)
            for t in range(nt):
                ps, psk = MM.next()
                dense_mm(ps, nr, hT_l(t), 8, wr(slot), 512, kfm(OFF_M0, t, nr) + [sk], psk)
                sc.op("act", lambda e, ps=ps, t=t, cg=cg: e.activation(out=gu[:nr, t, (cg - 6) * 512:(cg - 5) * 512], in_=ps[:nr, :],
                                                                       func=AF.Gelu_apprx_tanh), r=[psk], w=ktm(OFF_M3, t))
                pump(3)
        slot8, sk8 = load_w("w_in", 0, 8, 8 * 512, 512)
        slot9, sk9 = load_w("w_in", 0, 8, 9 * 512, 512)
        for t, tl in enumerate(tiles):
            gf, gfk = PB.next()
            for c2, (slot, sk) in enumerate(((slot8, sk8), (slot9, sk9))):
                ps, psk = MM.next()
                dense_mm(ps, nr, hT_l(t), 8, wr(slot), 512, kfm(OFF_M0, t, nr) + [sk], psk)
                sc.op("act", lambda e, ps=ps, gf=gf, c2=c2: e.activation(out=gf[:nr, c2 * 512:(c2 + 1) * 512], in_=ps[:nr, :],
                                                                         func=AF.Gelu_apprx_tanh), r=[psk], w=[(gfk, c2)])
            ss, ssk = PS.next()
            rs, rsk = PS.next()
            sc.op("act", lambda e, gf=gf, ss=ss: e.activation(out=junk[:nr, :], in_=gf[:nr, :], func=AF.Square, accum_out=ss[:nr, 0:1]),
                  r=[(gfk, 0), (gfk, 1)], w=[ssk, "junk"])
            rstd_ops((ss, ssk), (rs, rsk), D, 1, nr)
            if mode == "prompt":
                sc.op("dve", lambda e, gf=gf, rs=rs, t=t: e.scalar_tensor_tensor(out=gvb[:nr, t, :], in0=gf[:nr, :], scalar=rs[:nr, 0:1],
                                                                               in1=Gm[:nr, :], op0=ALU.mult, op1=ALU.mult),
                      r=[(gfk, 0), (gfk, 1), rsk, "Gm"], w=ktm(OFF_M4, t))
            else:
                sc.op("dve", lambda e, gf=gf, rs=rs: e.scalar_tensor_tensor(out=gf[:nr, :], in0=gf[:nr, :], scalar=rs[:nr, 0:1],
                                                                           in1=Gm[:nr, :], op0=ALU.mult, op1=ALU.mult),
                      r=[(gfk, 0), (gfk, 1), rsk, "Gm"], w=[(gfk, 0), (gfk, 1)])
                sc.op("pool", lambda e, gf=gf, tl=tl: e.dma_start(out=tl["gm"], in_=gf[:nr, :]), r=[(gfk, 0), (gfk, 1)], w=[],
                      dma="pbd%d" % gfk[1])
                sc.op("pool", lambda e, gf=gf, t=t: e.tensor_copy(out=gvb[:nr, t, :], in_=gf[:nr, :]), r=[(gfk, 0), (gfk, 1)],
                      w=ktm(OFF_M4, t))
            if t >= 2:
                p4_tile(t - 2)
        for t_ in range(max(0, nt - 2), nt):
            p4_tile(t_)
        stage('P3')
        flush_all()
        pump(100000)
        stage('P4')
        for c in range(2):
            s_ga, k_ga = load_w("w_in", 0, 8, (10 + c) * 512, 512)
            s_ab, k_ab = load_w("w_ab", 0, 8, c * 512, 512)
            s_gg, k_gg = load_w("w_in", 0, 8, (12 + c) * 512, 512)
            s_gb, k_gb = load_w("w_gb", 0, 8, c * 512, 512)
            for t in range(nt):
                t1, t1k = PA.next()
                t2, t2k = PA.next()
                ps, psk = MM.next()
                dense_mm(ps, nr, hT_l(t), 8, wr(s_ga), 512, kfm(OFF_M0, t, nr) + [k_ga], psk)
                sc.op("act", lambda e, ps=ps, t1=t1: e.activation(out=t1[:nr, :], in_=ps[:nr, :], func=AF.Sigmoid), r=[psk], w=[t1k])
                ps, psk = MM.next()
                dense_mm(ps, nr, lambda kc, t=t: oT[:, kc, t * nr:(t + 1) * nr], 8, wr(s_ab), 512, kfm(OFF_M2, t, nr) + [k_ab], psk)
                sc.op("dve", lambda e, ps=ps, t1=t1: e.tensor_tensor(out=t1[:nr, :], in0=ps[:nr, :], in1=t1[:nr, :], op=ALU.mult),
                      r=[psk, t1k], w=[t1k])
                ps, psk = MM.next()
                dense_mm(ps, nr, hT_l(t), 8, wr(s_gg), 512, kfm(OFF_M0, t, nr) + [k_gg], psk)
                sc.op("act", lambda e, ps=ps, t2=t2: e.activation(out=t2[:nr, :], in_=ps[:nr, :], func=AF.Sigmoid), r=[psk], w=[t2k])
                ps, psk = MM.next()
                dense_mm(ps, nr, lambda kc, t=t: gmT[:, kc, t * nr:(t + 1) * nr], 8, wr(s_gb), 512, kfm(OFF_M1, t, nr) + [k_gb], psk)
                sc.op("dve", lambda e, ps=ps, t2=t2: e.tensor_tensor(out=t2[:nr, :], in0=ps[:nr, :], in1=t2[:nr, :], op=ALU.mult),
                      r=[psk, t2k], w=[t2k])
                yb, ybk = PC.next()
                sc.op("dve", lambda e, t1=t1, t2=t2, yb=yb: e.tensor_tensor(out=yb[:nr, 0:512], in0=t1[:nr, :], in1=t2[:nr, :], op=ALU.add),
                      r=[t1k, t2k], w=[ybk])
                defer(lambda yb=yb, ybk=ybk, t=t, c=c: transposes(
                    yb, [ybk], nr, 4, lambda c0, n: yT[:, c * 4 + c0:c * 4 + c0 + n, t * nr:(t + 1) * nr], kfm(OFF_M5, t, nr)), 3)
        flush_all()
        stage('P5')
        for c in range(2):
            s_o, k_o = load_w("w_out", 0, 8, c * 512, 512)
            for t, tl in enumerate(tiles):
                if c == 0:
                    sc.op("sp", lambda e, t=t, tl=tl: e.dma_start(out=x1[:nr, t, :], in_=tl["x"]), w=kx1(t), dma="x1l%d" % t)
                ps, psk = MM.next()
                dense_mm(ps, nr, lambda kc, t=t: yT[:, kc, t * nr:(t + 1) * nr], 8, wr(s_o), 512, kfm(OFF_M5, t, nr) + [k_o], psk)
                sc.op("dve", lambda e, ps=ps, t=t, c=c: e.tensor_tensor(out=x1[:nr, t, c * 512:(c + 1) * 512], in0=ps[:nr, :],
                                                                        in1=x1[:nr, t, c * 512:(c + 1) * 512], op=ALU.add),
                      r=[psk] + kx1(t), w=kx1(t))
                if c == 1:
                    if t == 0:
                        ss6, ss6k = PS.next()
                        rs6, rs6k = PS.next()
                    sc.op("act", lambda e, t=t, ss6=ss6: e.activation(out=junk[:nr, :], in_=x1[:nr, t, :], func=AF.Square,
                                                                      accum_out=ss6[:nr, t:t + 1]), r=kx1(t), w=[(ss6k, t), "junk"])
        flush_all()
        rstd_ops((ss6, ss6k), (rs6, rs6k), D, nt, nr)
        for t in range(nt):
            xn, xnk = PC.next()
            sc.op("dve", lambda e, t=t, xn=xn: e.tensor_scalar(out=xn[:nr, :], in0=x1[:nr, t, :], scalar1=rs6[:nr, t:t + 1],
                                                               scalar2=None, op0=ALU.mult), r=kx1(t) + [rs6k], w=[xnk])
            transposes(xn, [xnk], nr, 8, lambda c0, n, t=t: hnT[:, c0:c0 + n, t * nr:(t + 1) * nr], kfm(OFF_M0, t, nr))
        stage('P6')
        if next_tiles is not None:
            lst = []
            nr2 = next_tiles[0]["nr"]
            for t2, tl2 in enumerate(next_tiles):
                xt2, xk2 = PB.next() if t2 < 3 else PA2.next()
                sc.op("sp", lambda e, xt2=xt2, tl2=tl2, nr2=nr2: e.dma_start(out=xt2[:nr2, :], in_=tl2["x"]), w=[xk2], dma="xld%d" % t2)
                lst.append((xt2, xk2))
            xpre[id(next_tiles)] = lst
        hk_all = [k for t in range(nt) for k in kfm(OFF_M0, t, nr)]
        for jg in range(0, NFC, 4):
            nj = min(4, NFC - jg)
            s_g, k_g = load_w("w_f1", 0, 8, jg * 128, nj * 128)
            s_u, k_u = load_w("w_f1", 0, 8, DFF + jg * 128, nj * 128)
            for jj in range(nj):
                j = jg + jj
                psg, psgk = MM.next()
                psu, psuk = MM.next()

                def ff(e, psg=psg, psu=psu, jj=jj, s_g=s_g, s_u=s_u):
                    ins = None
                    for ps, sl in ((psg, s_g), (psu, s_u)):
                        for kc in range(8):
                            ins = e.matmul(ps[:, 0:ntok], lhsT=sl[:, kc, jj * 128:(jj + 1) * 128], rhs=hnT[:, kc, 0:ntok],
                                           start=(kc == 0), stop=(kc == 7))
                    return ins
                sc.op("pe", ff, r=hk_all + [k_g, k_u], w=[psgk, psuk])
                flush()
                sg, sgk = PA.next()
                sc.op("act", lambda e, psg=psg, sg=sg: e.activation(out=sg[:, 0:ntok], in_=psg[:, 0:ntok], func=AF.Silu), r=[psgk], w=[sgk])
                sc.op("dve", lambda e, psu=psu, sg=sg, j=j: e.tensor_tensor(out=actT[:, j, 0:ntok], in0=psu[:, 0:ntok], in1=sg[:, 0:ntok],
                                                                          op=ALU.mult), r=[psuk, sgk], w=kact(j))
        stage('P7')
        ak_all = [k for j in range(NFC) for k in kact(j)]
        for c in range(2):
            sl = [load_w("w_f2", r0, min(8, NFC - r0), c * 512, 512) for r0 in range(0, NFC, 8)]
            pss = [MM.next() for _ in range(nt)]
            for si, (slot_, slk) in enumerate(sl):
                j0 = si * 8
                nj_ = min(8, NFC - j0)
                for t in range(nt):
                    ps, psk = pss[t]

                    def f2(e, ps=ps, t=t, slot_=slot_, j0=j0, nj_=nj_):
                        ins = None
                        for jj in range(nj_):
                            j = j0 + jj
                            ins = e.matmul(ps[:nr, :], lhsT=actT[:, j, t * nr:(t + 1) * nr], rhs=slot_[:, jj, :],
                                           start=(j == 0), stop=(j == NFC - 1))
                        return ins
                    sc.op("pe", f2, r=[k for j in range(j0, j0 + nj_) for k in kact(j)] + [slk], w=[psk])
                    flush()
            for t, tl in enumerate(tiles):
                ps, psk = pss[t]
                sc.op("dve", lambda e, ps=ps, t=t, c=c: e.tensor_tensor(out=x1[:nr, t, c * 512:(c + 1) * 512], in0=ps[:nr, :],
                                                                        in1=x1[:nr, t, c * 512:(c + 1) * 512], op=ALU.add),
                      r=[psk] + kx1(t), w=kx1(t))
                import os
                if c == 1 and not os.environ.get("DBG"):
                    sc.op("pool", lambda e, t=t, tl=tl: e.dma_start(out=tl["y"], in_=x1[:nr, t, :]), r=kx1(t), w=[], dma="x1s%d" % t)

    _stage[0] = 0
    try:
        setup()
        stage('setup')
        prepass()
        stage('prepass')
        plist = []
        for p in range(NP):
            for mt in range(S // 512):
                tiles = []
                for t in range(4):
                    r0 = p * S + mt * 512 + t * 128
                    tiles.append({"nr": 128, "x": xp[r0:r0 + 128, :], "y": yp[r0:r0 + 128, :], "k": kp[r0:r0 + 128, :],
                                  "v": vp[r0:r0 + 128, :]})
                plist.append((tiles, p, mt))
        stiles = []
        for s_ in range(NS):
            r0 = s_ * TS
            stiles.append({"nr": TS, "x": xs[r0:r0 + TS, :], "y": ys[r0:r0 + TS, :], "k": ksn[r0:r0 + TS, :],
                           "v": vsn[r0:r0 + TS, :], "gm": gms[r0:r0 + TS, :]})
        for i_, (tiles, p, mt) in enumerate(plist):
            if OVERLAP and p == 0 and mt == 0:
                p2_state["gen"] = part2_gen()
            if OVERLAP and p == 0 and mt == 2:
                p2_release()
            nxt_ = plist[i_ + 1][0] if i_ + 1 < len(plist) else (stiles if NS else None)
            macro(tiles, "prompt", seq=p, qi0=mt * 4, next_tiles=nxt_)
        if NS:
            sc.fence()
            for ui in range(2):
                vu = units[ui][3]
                sc.op("pool", lambda e, vu=vu: e.memset(vu[:, :, 128:129], 1.0), w=[("vuones", ui)])
            sc.op("pool", lambda e: e.memset(Vn[:, :, :, 128:129], 1.0), w=["Vnones"])
            tiles = stiles
            macro(tiles, "sample")

    except _Stop:
        deferred.clear()
    flush_all()
    sc.fence()
    sc.emit(nc)
    return nc


_CONST = {}


def _consts():
    if not _CONST:
        i = np.arange(LB)
        b = _bucket(127 - i)
        oh = np.zeros((32, LB), np.float32)
        oh[b, i] = 1.0
        _CONST["c_oh"] = oh
        _CONST["c_ident"] = np.eye(128, dtype=np.float32)
        _CONST["c_tril"] = np.tril(np.ones((128, 128), np.float32))
    return _CONST


_NC_CACHE = {}


def run_cfg(inputs, NP, S, NS, PAST, ncores):
    key = (NP, S, NS, PAST)
    if key not in _NC_CACHE:
        _NC_CACHE[key] = build(NP, S, NS, PAST)
    nc = _NC_CACHE[key]
    f = lambda a: np.ascontiguousarray(np.asarray(a, dtype=np.float32))
    shared = {
        "rel_table": f(inputs["rel_table"]), "norm1_g": f(inputs["norm1_g"]).reshape(1, D),
        "w_in": f(inputs["w_in"]).reshape(D, INW), "q_norm_g": f(inputs["q_norm_g"]).reshape(1, DK),
        "k_norm_g": f(inputs["k_norm_g"]).reshape(1, DK), "lambda_q1": f(inputs["lambda_q1"]).reshape(1, DK),
        "lambda_k1": f(inputs["lambda_k1"]).reshape(1, DK), "lambda_q2": f(inputs["lambda_q2"]).reshape(1, DK),
        "lambda_k2": f(inputs["lambda_k2"]).reshape(1, DK), "subln_g": f(inputs["subln_g"]).reshape(1, HD),
        "gm_norm_g": f(inputs["gm_norm_g"]).reshape(1, D), "gm_w_s": f(inputs["gm_w_s"]).reshape(8, 128, 128),
        "gm_b": f(inputs["gm_b"]).reshape(8, 128), "w_ab": f(inputs["w_attn_branch"]).reshape(D, D),
        "w_gb": f(inputs["w_gmlp_branch"]).reshape(D, D), "w_out": f(inputs["w_out"]).reshape(D, D),
        "norm2_g": f(inputs["norm2_g"]).reshape(1, D), "w_f1": f(inputs["w_ffn_in"]).reshape(D, 2 * DFF),
        "w_f2": f(inputs["w_ffn_out"]).reshape(DFF, D),
    }
    shared.update(_consts())
    xp_ = f(inputs["x_prompt"])
    xs_ = f(inputs["x_sample"])
    ck_ = f(inputs["cache_k"])[0]
    cv_ = f(inputs["cache_v"])[0]
    in_maps = []
    for c in range(ncores):
        m = dict(shared)
        m["xp"] = xp_[c * NP:(c + 1) * NP].reshape(NP * S, D)
        m["xs"] = xs_[c * NS:(c + 1) * NS].reshape(NS * 64, D)
        m["ck"] = ck_[c * NS:(c + 1) * NS].reshape(NS, PAST, D)
        m["cv"] = cv_[c * NS:(c + 1) * NS].reshape(NS, PAST, D)
        in_maps.append(m)
    res = run_bass_kernel_spmd(nc, in_maps, core_ids=list(range(ncores)))
    R = res.results
    cat = lambda n: np.concatenate([r[n] for r in R], axis=0)
    B, Bs = NP * ncores, NS * ncores
    y_p = cat("yp").reshape(B, S, D)
    y_s = cat("ys").reshape(Bs, 64, D)
    k_p = cat("kp").reshape(1, B, S, H, 2, DK)
    v_p = cat("vp").reshape(1, B, S, H, HD)
    k_s = cat("ksn").reshape(1, Bs, 64, H, 2, DK)
    v_s = cat("vsn").reshape(1, Bs, 64, H, HD)
    g_s = cat("gms").reshape(1, Bs, 64, D)
    return (y_p, y_s, k_p, v_p, k_s, v_s, g_s)


def kernel(**inputs):
    return run_cfg(inputs, 2, 2048, 4, 2048, NCORES)
```

```python
import math
import numpy as np
import concourse.bass as bass
import concourse.mybir as mybir
from concourse.bass_utils import run_bass_kernel_spmd

F32 = mybir.dt.float32
BF16 = mybir.dt.bfloat16
AF = mybir.ActivationFunctionType
ALU = mybir.AluOpType
AX = mybir.AxisListType

D = 1024
H = 8
DK = 64
HD = 128
DFF = 2816
NFC = DFF // 128
INW = 7168
EPS = 1e-6
SCALE = DK ** -0.5
LAM_INIT = 0.8 - 0.6 * math.exp(0.0)
LB = 384
NCORES = 8


def _bucket(rel):
    half = 16
    max_exact = 8
    ret = np.where(rel > 0, half, 0)
    n = np.abs(rel)
    nf = np.maximum(n, 1).astype(np.float32)
    large = max_exact + (np.log(nf / np.float32(max_exact)) / np.float32(math.log(128 / max_exact))
                         * (half - max_exact)).astype(np.int32)
    large = np.minimum(large, half - 1)
    return ret + np.where(n < max_exact, n, large)


class _Stop(Exception):
    pass


STOP = None
_stage = [0]


def stage(name):
    _stage[0] += 1
    if STOP is not None and _stage[0] >= STOP:
        print("STOP at stage", _stage[0], name)
        raise _Stop()


class Sched:
    ENGS = ("pe", "act", "dve", "pool", "sp")

    def __init__(self):
        self.ops = {e: [] for e in self.ENGS}
        self.last_w = {}
        self.readers = {}
        self.dma_cnt = {}
        self.subs = {}

    @staticmethod
    def _base(k):
        if isinstance(k, tuple) and len(k) == 2 and isinstance(k[0], tuple):
            return k[0]
        return None

    def _expand(self, k):
        out = [k]
        b = self._base(k)
        if b is not None:
            self.subs.setdefault(b, set()).add(k)
            out.append(b)
        out += list(self.subs.get(k, ()))
        return out

    def op(self, eng, fn, r=(), w=(), dma=None):
        deps = set()
        for k in r:
            for x in self._expand(k):
                t = self.last_w.get(x)
                if t is not None:
                    deps.add(t)
        for k in w:
            for x in self._expand(k):
                t = self.last_w.get(x)
                if t is not None:
                    deps.add(t)
                for t in self.readers.get(x, ()):
                    deps.add(t)
        idx = len(self.ops[eng])
        if dma is not None:
            prev = self.dma_cnt.get(dma, 0)
            if prev:
                deps.add(("d", dma, prev))
            cnt = prev + 16
            self.dma_cnt[dma] = cnt
            tok = ("d", dma, cnt)
        else:
            tok = ("e", eng, idx)
        if eng == "pe":
            deps = {t for t in deps if not (t[0] == "e" and t[1] == "pe")}
        self.ops[eng].append({"fn": fn, "deps": deps, "dma": dma, "sig": False})
        for k in r:
            self.readers.setdefault(k, []).append(tok)
        for k in w:
            self.last_w[k] = tok
            self.readers[k] = []
            for x in self.subs.get(k, ()):
                self.last_w[x] = tok
                self.readers[x] = []
        return tok

    def fence(self):
        toks = []
        for e in self.ENGS:
            if self.ops[e]:
                toks.append(("e", e, len(self.ops[e]) - 1))
        for name, cnt in self.dma_cnt.items():
            toks.append(("d", name, cnt))
        key = ("fence", len(toks), sum(len(v) for v in self.ops.values()))
        for e in self.ENGS:
            self.ops[e].append({"fn": None, "deps": set(toks), "dma": None, "sig": False})

    def emit(self, nc):
        for e in self.ENGS:
            for o in self.ops[e]:
                for t in o["deps"]:
                    if t[0] == "e":
                        self.ops[t[1]][t[2]]["sig"] = True
        cum = {}
        for e in self.ENGS:
            c = 0
            arr = []
            ops = self.ops[e]
            for i, o in enumerate(ops):
                if o["sig"] and o["fn"] is None:
                    o["sig"] = False
                    j = i - 1
                    while j >= 0 and (ops[j]["fn"] is None or ops[j]["dma"] is not None):
                        j -= 1
                    if j >= 0:
                        ops[j]["sig"] = True
            for o in ops:
                if o["sig"] and o["dma"] is None and o["fn"] is not None:
                    c += 1
                arr.append(c)
            cum[e] = arr
        dma_names = sorted(self.dma_cnt.keys())
        import contextlib
        with contextlib.ExitStack() as st:
            esem = {e: st.enter_context(nc.semaphore("s_" + e)) for e in self.ENGS}
            dsem = {n: st.enter_context(nc.semaphore("d_" + n)) for n in dma_names}
            block = st.enter_context(nc.Block())

            def run(ename, eng):
                waited = {}
                for o in self.ops[ename]:
                    need = {}
                    for t in o["deps"]:
                        if t[0] == "e":
                            val = cum[t[1]][t[2]]
                            if val == 0:
                                continue
                            key = ("e", t[1])
                            need[key] = max(need.get(key, 0), val)
                        else:
                            key = ("d", t[1])
                            need[key] = max(need.get(key, 0), t[2])
                    import os
                    if os.environ.get("DUMP"):
                        print("OP", ename, self.ops[ename].index(o), "needs", {k: v for k, v in need.items() if waited.get(k, 0) < v},
                              "sig" if o["sig"] else "", "dma=%s" % o["dma"] if o["dma"] else "", "nop" if o["fn"] is None else "")
                    for key, val in need.items():
                        if waited.get(key, 0) < val:
                            sem = esem[key[1]] if key[0] == "e" else dsem[key[1]]
                            eng.wait_ge(sem, val)
                            waited[key] = val
                    if o["fn"] is None:
                        continue
                    ins = o["fn"](eng)
                    if o["dma"] is not None:
                        ins.then_inc(dsem[o["dma"]], 16)
                    elif o["sig"]:
                        ins.then_inc(esem[ename], 1)
                if ename == "sp":
                    for n in dma_names:
                        eng.wait_ge(dsem[n], self.dma_cnt[n])

            @block.tensor
            def _(e):
                run("pe", e)

            @block.scalar
            def _(e):
                run("act", e)

            @block.vector
            def _(e):
                run("dve", e)

            @block.gpsimd
            def _(e):
                run("pool", e)

            @block.sync
            def _(e):
                run("sp", e)


class Ring:
    def __init__(self, name, aps):
        self.name = name
        self.aps = aps
        self.i = 0

    def next(self):
        j = self.i % len(self.aps)
        self.i += 1
        return self.aps[j], (self.name, j)


def build(NP, S, NS, PAST):
    assert S % 512 == 0 and PAST % 128 == 0 and NS <= 4
    nc = bass.Bass("TRN2", target_bir_lowering=False)
    sc = Sched()
    NKT = S // 128
    NJ = PAST // 128
    TS = 64

    def din(name, shape):
        return nc.dram_tensor(name, shape, F32, kind="ExternalInput").ap()

    def dout(name, shape):
        return nc.dram_tensor(name, shape, F32, kind="ExternalOutput").ap()

    xp = din("xp", [NP * S, D])
    xs = din("xs", [NS * TS, D])
    ck = din("ck", [NS, PAST, D])
    cv = din("cv", [NS, PAST, D])
    rel_table = din("rel_table", [32, H])
    norm1_g = din("norm1_g", [1, D])
    w_in = din("w_in", [D, INW])
    q_norm_g = din("q_norm_g", [1, DK])
    k_norm_g = din("k_norm_g", [1, DK])
    lq1 = din("lambda_q1", [1, DK])
    lk1 = din("lambda_k1", [1, DK])
    lq2 = din("lambda_q2", [1, DK])
    lk2 = din("lambda_k2", [1, DK])
    subln_g = din("subln_g", [1, HD])
    gm_norm_g = din("gm_norm_g", [1, D])
    gm_w_s = din("gm_w_s", [8, 128, 128])
    gm_b = din("gm_b", [8, 128])
    w_ab = din("w_ab", [D, D])
    w_gb = din("w_gb", [D, D])
    w_out = din("w_out", [D, D])
    norm2_g = din("norm2_g", [1, D])
    w_f1 = din("w_f1", [D, 2 * DFF])
    w_f2 = din("w_f2", [DFF, D])
    c_ident = din("c_ident", [128, 128])
    c_oh = din("c_oh", [32, LB])
    c_tril = din("c_tril", [128, 128])

    yp = dout("yp", [NP * S, D])
    ys = dout("ys", [NS * TS, D])
    kp = dout("kp", [NP * S, D])
    vp = dout("vp", [NP * S, D])
    ksn = dout("ksn", [NS * TS, D])
    vsn = dout("vsn", [NS * TS, D])
    gms = dout("gms", [NS * TS, D])

    wsrc = {"w_in": (w_in, D, INW), "w_ab": (w_ab, D, D), "w_gb": (w_gb, D, D), "w_out": (w_out, D, D),
            "w_f1": (w_f1, D, 2 * DFF), "w_f2": (w_f2, DFF, D)}
    wb = {n: nc.dram_tensor("wb_" + n, [r, c], BF16).ap() for n, (_, r, c) in wsrc.items()}
    rscr = nc.dram_tensor("rscr", [H, 128, LB], F32).ap()

    def sb(name, shape, dt):
        return nc.alloc_sbuf_tensor(name, shape, dt)

    RES_N = max(8 * S + NKT * 8 * 129, 2 * (PAST * 4 + PAST + NJ * 129) + 8 * NS * 64 + NS * 8 * 129 + 64)
    RES = sb("RES", [128, RES_N], BF16)
    KT = RES[:, 0:8 * S].rearrange("p (h k) -> p h k", h=8)
    Vr = RES[:, 8 * S:8 * S + NKT * 8 * 129].rearrange("p (t h c) -> p t h c", t=NKT, h=8)
    units = []
    off = 0
    for u in range(2):
        kst = RES[:, off:off + 2 * PAST].bitcast(F32).rearrange("p (j c) -> p j c", c=128)
        off += 2 * PAST
        vst = RES[:, off:off + 2 * PAST].bitcast(F32).rearrange("p (j c) -> p j c", c=128)
        off += 2 * PAST
        ktu = RES[:, off:off + PAST]
        off += PAST
        vu = RES[:, off:off + NJ * 129].rearrange("p (j c) -> p j c", c=129)
        off += NJ * 129
        off += off % 2
        units.append((kst, vst, ktu, vu))
    KTn = RES[:, off:off + 8 * NS * 64].rearrange("p (h k) -> p h k", h=8)
    off += 8 * NS * 64
    Vn = RES[:, off:off + NS * 8 * 129].rearrange("p (s h c) -> p s h c", s=NS, h=8)
    off += NS * 8 * 129

    MA = sb("MA", [128, 6 * 4096], BF16)
    OFF_M1, OFF_M2, OFF_M5, OFF_M3, OFF_M4, OFF_M0 = 0, 4096, 8192, 12288, 16384, 20480

    def fm(offs):
        return MA[:, offs:offs + 4096].rearrange("p (c n) -> p c n", c=8)

    hT = fm(OFF_M0)
    qT = fm(OFF_M1)
    oT = fm(OFF_M2)
    gmT = qT
    yT = fm(OFF_M5)
    hnT = hT
    gu = MA[:, OFF_M3:OFF_M3 + 4096].rearrange("p (t n) -> p t n", t=4)
    gvb = MA[:, OFF_M4:OFF_M4 + 4096].rearrange("p (t n) -> p t n", t=4)
    x1 = MA[:, OFF_M3:OFF_M3 + 8192].bitcast(F32).rearrange("p (t n) -> p t n", t=4)
    actT = MA[:, 0:NFC * 512].rearrange("p (j n) -> p j n", j=NFC)

    def kfm(offs, t, nr):
        return [("MA", (offs + c * 512 + t * nr) // 128) for c in range(8)]

    def ktm(offs, t):
        return [("MA", (offs + t * 1024) // 128 + i) for i in range(8)]

    def kx1(t):
        return [("MA", (OFF_M3 + t * 2048) // 128 + i) for i in range(16)]

    def kact(j):
        return [("MA", (j * 512) // 128 + i) for i in range(4)]

    NSLOT = 4
    WR = sb("WR", [128, NSLOT * 4096], BF16)
    wring = Ring("wr", [WR[:, i * 4096:(i + 1) * 4096].rearrange("p (k n) -> p k n", k=8) for i in range(NSLOT)])
    NPA, NPB, NPC, NPT, NPS = 8, 3, 4, 6, 24
    PAt = sb("PA", [128, NPA * 512], F32)
    PA = Ring("pa", [PAt[:, i * 512:(i + 1) * 512] for i in range(NPA - 2)])
    PA2 = Ring("pa2", [PAt[:, (NPA - 2) * 512:NPA * 512]])
    PBt = sb("PB", [128, NPB * 1024], F32)
    PB = Ring("pb", [PBt[:, i * 1024:(i + 1) * 1024] for i in range(NPB)])
    PCt = sb("PC", [128, NPC * 1024], BF16)
    PC = Ring("pc", [PCt[:, i * 1024:(i + 1) * 1024] for i in range(NPC)])
    PTt = sb("PT", [128, NPT * 512], BF16)
    PT = Ring("pt", [PTt[:, i * 1024:(i + 1) * 1024] for i in range(NPT // 2)])
    PSt = sb("PS", [128, NPS * 8], F32)
    PS = Ring("ps", [PSt[:, i * 8:(i + 1) * 8] for i in range(NPS)])
    junk = sb("junk", [128, 1024], BF16)

    identf = sb("identf", [128, 128], F32)
    identb = sb("identb", [128, 128], BF16)
    biasT = sb("biasT", [128, H, 256], F32)
    cfar = sb("cfar", [128, H], F32)
    neg_lam = sb("neg_lam", [128, 1], F32)
    gqs = sb("gqs", [128, DK], F32)
    gkb = sb("gkb", [128, DK], F32)
    Gm = sb("Gm", [128, D], F32)
    g1col = sb("g1col", [128, 8], F32)
    g2col = sb("g2col", [128, 8], F32)
    sgcol = sb("sgcol", [128, 1], F32)
    wmT = sb("wmT", [128, 8, 128], BF16)
    bcol = sb("bcol", [128, 8], F32)
    gqcol = sb("gqcol", [128, 1], F32)

    NMM = 6
    MMall = nc.alloc_psum_tensor("mmall", [128, NMM * 512], F32)
    MM = Ring("mm", [MMall[:, i * 512:(i + 1) * 512] for i in range(NMM)])
    ACCall = nc.alloc_psum_tensor("accall", [128, 1024], F32)
    ACCt = [ACCall[:, i * 512:(i + 1) * 512] for i in range(2)]
    class _TR:
        @staticmethod
        def next():
            ap, k = MM.next()
            return ap.bitcast(BF16)[:, 0:512], k
    TR = _TR()
    acc_i = [0]

    deferred = []
    dbg_y = [None]

    def defer(fn, delay=1):
        deferred.append([delay, fn])

    def flush():
        for ent in deferred:
            ent[0] -= 1
        ready = [ent for ent in deferred if ent[0] <= 0]
        for ent in ready:
            deferred.remove(ent)
        for ent in ready:
            ent[1]()

    def flush_all():
        while deferred:
            deferred.pop(0)[1]()

    def rstd_ops(ss, rs, n, width, nr):
        ssa, ssk = ss
        rsa, rsk = rs
        sc.op("act", lambda e: e.activation(out=rsa[:nr, 0:width], in_=ssa[:nr, 0:width], func=AF.Ln,
                                            scale=1.0 / n, bias=EPS), r=[ssk], w=[rsk])
        sc.op("act", lambda e: e.activation(out=rsa[:nr, 0:width], in_=rsa[:nr, 0:width], func=AF.Exp,
                                            scale=-0.5), r=[rsk], w=[rsk])

    evac_flip = [0]

    def evac(out_ap, in_ap, r, w):
        evac_flip[0] ^= 1
        import os
        mode_ = os.environ.get("EVAC", "both")
        if (evac_flip[0] and mode_ == "both") or mode_ == "act":
            sc.op("act", lambda e: e.activation(out=out_ap, in_=in_ap, func=AF.Copy), r=r, w=w)
        else:
            sc.op("dve", lambda e: e.tensor_copy(out=out_ap, in_=in_ap), r=r, w=w)

    def transposes(src, srck, nr, nchunks, dst_fn, dstk, ident=None, scale=None, force_act=False):
        for g0 in range(0, nchunks, 4):
            n = min(4, nchunks - g0)
            tp, tpk = TR.next()

            def f(e, g0=g0, n=n, tp=tp):
                ins = None
                for j in range(n):
                    ins = e.transpose(out=tp[:, j * nr:(j + 1) * nr], in_=src[:nr, (g0 + j) * 128:(g0 + j + 1) * 128],
                                      identity=identb[:nr, :nr])
                return ins
            sc.op("pe", f, r=list(srck) + ["identb"], w=[tpk])
            import os
            tv = os.environ.get("TRV", "3d")
            if scale is not None:
                d__ = dst_fn(g0, n)
                sc.op("act", lambda e, d__=d__, tp=tp, n=n: e.activation(
                    out=d__, in_=tp[:, 0:n * nr].rearrange("p (c n) -> p c n", c=n), func=AF.Copy, scale=scale[:, 0:1]),
                    r=[tpk, "gqcol"], w=dstk)
            elif force_act:
                d__ = dst_fn(g0, n)
                sc.op("act", lambda e, d__=d__, tp=tp, n=n: e.activation(
                    out=d__, in_=tp[:, 0:n * nr].rearrange("p (c n) -> p c n", c=n), func=AF.Copy), r=[tpk], w=dstk)
            elif tv == "3d":
                evac(dst_fn(g0, n), tp[:, 0:n * nr].rearrange("p (c n) -> p c n", c=n), r=[tpk], w=dstk)
            elif tv == "2d":
                d_ = dst_fn(g0, n)
                for j in range(n):
                    evac(d_[:, j, :], tp[:, j * nr:(j + 1) * nr], r=[tpk], w=dstk)

    def load_w(name, rc0, nrc, col0, ncols):
        slot, sk = wring.next()
        src = wb[name][rc0 * 128:(rc0 + nrc) * 128, col0:col0 + ncols].rearrange("(k p) n -> p k n", p=128)
        rk = [("wb", name, rc0 + i, cb) for i in range(nrc) for cb in range(col0 // 1024, (col0 + ncols - 1) // 1024 + 1)]
        sc.op("sp", lambda e: e.dma_start(out=slot[:, 0:nrc, 0:ncols], in_=src), r=rk, w=[sk], dma="wr%d" % sk[1])
        return slot, sk

    def dense_mm(ps, nr, lhs_fn, nk, rhs_fn, ncols, r, psk):
        def f(e):
            ins = None
            for kc in range(nk):
                ins = e.matmul(ps[:nr, 0:ncols], lhsT=lhs_fn(kc), rhs=rhs_fn(kc), start=(kc == 0), stop=(kc == nk - 1))
            return ins
        sc.op("pe", f, r=r, w=[psk])
        flush()

    def setup():
        sc.op("sp", lambda e: e.dma_start(out=identf[:], in_=c_ident[:, :]), w=["identf"], dma="c0")
        sc.op("dve", lambda e: e.tensor_copy(out=identb[:], in_=identf[:]), r=["identf"], w=["identb"])
        sc.op("sp", lambda e: e.dma_start(out=gkb[:], in_=k_norm_g[0:1, :].partition_broadcast(128)), w=["gkb"], dma="c1")
        sc.op("sp", lambda e: e.dma_start(out=gqs[:], in_=q_norm_g[0:1, :].partition_broadcast(128)), w=["gqs"], dma="c2")
        sc.op("dve", lambda e: e.tensor_scalar(out=gqs[:], in0=gqs[:], scalar1=SCALE, scalar2=None, op0=ALU.mult),
              r=["gqs"], w=["gqs"])
        sc.op("sp", lambda e: e.dma_start(out=Gm[:], in_=gm_norm_g[0:1, :].partition_broadcast(128)), w=["Gm"], dma="c3")
        sc.op("sp", lambda e: e.dma_start(out=g1col[:], in_=norm1_g.rearrange("o (c p) -> p (o c)", p=128),
                                          allow_slow_non_contiguous=True), w=["g1col"], dma="c4")
        sc.op("sp", lambda e: e.dma_start(out=g2col[:], in_=norm2_g.rearrange("o (c p) -> p (o c)", p=128),
                                          allow_slow_non_contiguous=True), w=["g2col"], dma="c5")
        sc.op("sp", lambda e: e.dma_start(out=sgcol[:], in_=subln_g.rearrange("o p -> p o"),
                                          allow_slow_non_contiguous=True), w=["sgcol"], dma="c6")
        sc.op("dve", lambda e: e.tensor_scalar(out=sgcol[:], in0=sgcol[:], scalar1=1.0 - LAM_INIT, scalar2=None,
                                               op0=ALU.mult), r=["sgcol"], w=["sgcol"])
        for hf in range(2):
            sc.op("sp", lambda e, hf=hf: e.dma_start(out=gqcol[hf * 64:(hf + 1) * 64, :], in_=q_norm_g.rearrange("o d -> d o"),
                                                    allow_slow_non_contiguous=True), w=[("gqcol_", hf)], dma="c16")
        sc.op("dve", lambda e: e.tensor_scalar(out=gqcol[:], in0=gqcol[:], scalar1=SCALE, scalar2=None, op0=ALU.mult),
              r=[("gqcol_", 0), ("gqcol_", 1)], w=["gqcol"])
        sc.op("sp", lambda e: e.dma_start(out=bcol[:], in_=gm_b.rearrange("g t -> t g"),
                                          allow_slow_non_contiguous=True), w=["bcol"], dma="c7")
        stage('s_loads')
        lt, ltk = PA.next()
        for i, src in enumerate((lq1, lk1, lq2, lk2)):
            sc.op("sp", lambda e, i=i, src=src: e.dma_start(out=lt[:, i * 64:(i + 1) * 64],
                                                            in_=src[0:1, :].partition_broadcast(128)),
                  w=[(ltk, i)], dma="c8")
        st, stk = PS.next()
        pr, prk = PA.next()
        sc.op("dve", lambda e: e.tensor_tensor(out=pr[:, 0:64], in0=lt[:, 0:64], in1=lt[:, 64:128], op=ALU.mult),
              r=[(ltk, 0), (ltk, 1)], w=[(prk, 0)])
        sc.op("dve", lambda e: e.tensor_tensor(out=pr[:, 64:128], in0=lt[:, 128:192], in1=lt[:, 192:256], op=ALU.mult),
              r=[(ltk, 2), (ltk, 3)], w=[(prk, 1)])
        sc.op("dve", lambda e: e.tensor_reduce(out=st[:, 0:2], in_=pr[:, 0:128].rearrange("p (a d) -> p a d", a=2),
                                               axis=AX.X, op=ALU.add), r=[(prk, 0), (prk, 1)], w=[stk])
        sc.op("act", lambda e: e.activation(out=st[:, 2:4], in_=st[:, 0:2], func=AF.Exp), r=[stk], w=[(stk, 1)])
        sc.op("dve", lambda e: e.tensor_tensor(out=neg_lam[:], in0=st[:, 3:4], in1=st[:, 2:3], op=ALU.subtract),
              r=[(stk, 1)], w=["neg_lam"])
        sc.op("dve", lambda e: e.tensor_scalar(out=neg_lam[:], in0=neg_lam[:], scalar1=-LAM_INIT, scalar2=None,
                                               op0=ALU.add), r=["neg_lam"], w=["neg_lam"])
        stage('s_lambda')
        ws, wsk = PB.next()
        ws3 = ws.rearrange("p (g s) -> p g s", g=8)
        tr_, trk = PA.next()
        sc.op("sp", lambda e: e.dma_start(out=ws3, in_=gm_w_s.rearrange("g t s -> t g s")), w=[wsk], dma="c9")
        sc.op("sp", lambda e: e.dma_start(out=tr_[:, 0:128], in_=c_tril[:, :]), w=[trk], dma="c10")
        wmb, wmbk = PC.next()
        sc.op("dve", lambda e: e.tensor_tensor(out=wmb.rearrange("p (g s) -> p g s", g=8), in0=ws3,
                                               in1=tr_[:, 0:128].unsqueeze(1).to_broadcast([128, 8, 128]), op=ALU.mult),
              r=[wsk, trk], w=[wmbk])
        stage('s_gmlp_tt')
        transposes(wmb, [wmbk], 128, 8, lambda c0, n: wmT[:, c0:c0 + n, :], ["wmT"])
        stage('s_gmlp')
        tb, tbk = PA.next()
        oh, ohk = PA.next()
        sc.op("sp", lambda e: e.dma_start(out=tb[0:32, 0:8], in_=rel_table[:, :]), w=[tbk], dma="c11")
        sc.op("sp", lambda e: e.dma_start(out=oh[0:32, 0:LB], in_=c_oh[:, :]), w=[ohk], dma="c12")
        ohb, ohbk = PC.next()
        sc.op("dve", lambda e: e.tensor_copy(out=ohb[0:32, 0:LB], in_=oh[0:32, 0:LB]), r=[ohk], w=[ohbk])
        tB, tBk = PB.next()
        tB3 = tB.rearrange("p (h n) -> p h n", h=8)
        sc.op("dve", lambda e: e.tensor_copy(out=tB3[0:32], in_=tb[0:32, 0:8].unsqueeze(2).to_broadcast([32, 8, 128])),
              r=[tbk], w=[tBk])
        thi, thik = PC.next()
        tlo, tlok = PC.next()
        tres, tresk = PB.next()
        sc.op("dve", lambda e: e.tensor_copy(out=thi[0:32, :], in_=tB[0:32, :]), r=[tBk], w=[thik])
        sc.op("dve", lambda e: e.tensor_copy(out=tres[0:32, :], in_=thi[0:32, :]), r=[thik], w=[tresk])
        sc.op("dve", lambda e: e.tensor_tensor(out=tres[0:32, :], in0=tB[0:32, :], in1=tres[0:32, :], op=ALU.subtract),
              r=[tBk, tresk], w=[tresk])
        sc.op("dve", lambda e: e.tensor_copy(out=tlo[0:32, :], in_=tres[0:32, :]), r=[tresk], w=[tlok])
        for h in range(H):
            ps, psk = MM.next()

            def f(e, h=h, ps=ps):
                e.matmul(ps[:, 0:LB], lhsT=thi[0:32, h * 128:(h + 1) * 128], rhs=ohb[0:32, 0:LB], start=True, stop=False)
                return e.matmul(ps[:, 0:LB], lhsT=tlo[0:32, h * 128:(h + 1) * 128], rhs=ohb[0:32, 0:LB], start=False, stop=True)
            sc.op("pe", f, r=[thik, tlok, ohbk], w=[psk])
            rr, rrk = PA.next()
            sc.op("act", lambda e, ps=ps, rr=rr: e.activation(out=rr[:, 0:LB], in_=ps[:, 0:LB], func=AF.Copy), r=[psk], w=[rrk])
            sc.op("dve", lambda e, h=h, rr=rr: e.tensor_copy(out=cfar[:, h:h + 1], in_=rr[:, LB - 1:LB]), r=[rrk], w=[("cfar", h)])
            sc.op("sp", lambda e, h=h, rr=rr: e.dma_start(out=rscr[h, :, :], in_=rr[:, 0:LB]), r=[rrk], w=[("rscr", h)], dma="c13")
        stage('s_biasmm')
        src_d = bass.AP(tensor=rscr.tensor, offset=127, ap=[[LB - 1, 128], [128 * LB, H], [1, 128]])
        src_s = bass.AP(tensor=rscr.tensor, offset=127 + 128, ap=[[LB - 1, 128], [128 * LB, H], [1, 128]])
        rk = [("rscr", h) for h in range(H)]
        sc.op("sp", lambda e: e.dma_start(out=biasT[:, :, 128:256], in_=src_d), r=rk, w=["biasT_d"], dma="c14")
        sc.op("sp", lambda e: e.dma_start(out=biasT[:, :, 0:128], in_=src_s), r=rk, w=["biasT_s"], dma="c15")
        stage('s_skew')
        sc.op("pool", lambda e: e.memset(biasT[64:128, :, 128:192], -30000.0), r=[], w=["biasT_d"])
        sc.op("pool", lambda e: e.memset(Vr[:, :, :, 128:129], 1.0), w=["Vones"])

    def prepass():
        NST = 4
        stf = [MA[:, i * 2048:(i + 1) * 2048].bitcast(F32) for i in range(NST)]
        stb = [MA[:, NST * 2048 + i * 1024:NST * 2048 + (i + 1) * 1024] for i in range(NST)]
        kf = [[("MA", i * 16 + b) for b in range(16)] for i in range(NST)]
        kb = [[("MA", NST * 16 + i * 8 + b) for b in range(8)] for i in range(NST)]
        cnt = [0]

        def piece(name, rc, cb, ncols, gain):
            src, R, C = wsrc[name]
            i = cnt[0] % NST
            on_act = (cnt[0] % 2 == 0)
            cnt[0] += 1
            f_, b_ = stf[i], stb[i]
            c0 = cb * 1024
            sc.op("sp", lambda e: e.dma_start(out=f_[:, 0:ncols], in_=src[rc * 128:(rc + 1) * 128, c0:c0 + ncols]),
                  w=kf[i], dma="pf%d" % i)
            if gain is None:
                if on_act:
                    sc.op("act", lambda e: e.activation(out=b_[:, 0:ncols], in_=f_[:, 0:ncols], func=AF.Copy), r=kf[i], w=kb[i])
                else:
                    sc.op("dve", lambda e: e.tensor_copy(out=b_[:, 0:ncols], in_=f_[:, 0:ncols]), r=kf[i], w=kb[i])
            else:
                gap, gk = gain
                if on_act:
                    sc.op("act", lambda e: e.activation(out=b_[:, 0:ncols], in_=f_[:, 0:ncols], func=AF.Copy, scale=gap),
                          r=kf[i] + [gk], w=kb[i])
                else:
                    sc.op("dve", lambda e: e.tensor_scalar(out=b_[:, 0:ncols], in0=f_[:, 0:ncols], scalar1=gap, scalar2=None,
                                                           op0=ALU.mult), r=kf[i] + [gk], w=kb[i])
            sc.op("pool", lambda e: e.dma_start(out=wb[name][rc * 128:(rc + 1) * 128, c0:c0 + ncols], in_=b_[:, 0:ncols]),
                  r=kb[i], w=[("wb", name, rc, cb)], dma="pb%d" % i)

        for cb in range(INW // 1024):
            for rc in range(8):
                piece("w_in", rc, cb, 1024, (g1col[:, rc:rc + 1], "g1col"))
        if OVERLAP:
            return
        for name, rc, cb, ncols, gain in part2_list():
            piece(name, rc, cb, ncols, gain)

    def part2_list():
        out = []
        for rc in range(8):
            out.append(("w_ab", rc, 0, 1024, (sgcol[:, 0:1], "sgcol")))
        for rc in range(8):
            out.append(("w_gb", rc, 0, 1024, None))
        for rc in range(8):
            out.append(("w_out", rc, 0, 1024, None))
        ncb = (2 * DFF + 1023) // 1024
        for cb in range(ncb):
            for rc in range(8):
                out.append(("w_f1", rc, cb, min(1024, 2 * DFF - cb * 1024), (g2col[:, rc:rc + 1], "g2col")))
        for rc in range(NFC):
            out.append(("w_f2", rc, 0, 1024, None))
        return out

    OVERLAP = (S >= 2048)
    p2_state = {"gen": None}

    def part2_gen():
        NS2 = 5
        f2 = [KT[:, i, 1024:2048].bitcast(F32) for i in range(NS2)]
        b2 = [KT[:, 5 + i // 2, 1024 + (i % 2) * 512:1024 + (i % 2) * 512 + 512] for i in range(NS2)]
        cnt = 0
        for name, rc, cb, ncols, gain in part2_list():
            src, R, C = wsrc[name]
            for half in range((ncols + 511) // 512):
                c0 = cb * 1024 + half * 512
                nc_ = min(512, cb * 1024 + ncols - c0)
                i = cnt % NS2
                on_act = (cnt % 2 == 0)
                cnt += 1
                f_, b_ = f2[i], b2[i]
                sc.op("sp", lambda e, f_=f_, src=src, rc=rc, c0=c0, nc_=nc_: e.dma_start(
                    out=f_[:, 0:nc_], in_=src[rc * 128:(rc + 1) * 128, c0:c0 + nc_]), w=[("st2f", i)], dma="p2f%d" % i)
                if gain is None:
                    if on_act:
                        sc.op("act", lambda e, f_=f_, b_=b_, nc_=nc_: e.activation(out=b_[:, 0:nc_], in_=f_[:, 0:nc_], func=AF.Copy),
                              r=[("st2f", i)], w=[("st2b", i)])
                    else:
                        sc.op("dve", lambda e, f_=f_, b_=b_, nc_=nc_: e.tensor_copy(out=b_[:, 0:nc_], in_=f_[:, 0:nc_]),
                              r=[("st2f", i)], w=[("st2b", i)])
                else:
                    gap, gk = gain
                    if on_act:
                        sc.op("act", lambda e, f_=f_, b_=b_, nc_=nc_, gap=gap: e.activation(
                            out=b_[:, 0:nc_], in_=f_[:, 0:nc_], func=AF.Copy, scale=gap), r=[("st2f", i), gk], w=[("st2b", i)])
                    else:
                        sc.op("dve", lambda e, f_=f_, b_=b_, nc_=nc_, gap=gap: e.tensor_scalar(
                            out=b_[:, 0:nc_], in0=f_[:, 0:nc_], scalar1=gap, scalar2=None, op0=ALU.mult),
                            r=[("st2f", i), gk], w=[("st2b", i)])
                sc.op("pool", lambda e, b_=b_, name=name, rc=rc, c0=c0, nc_=nc_: e.dma_start(
                    out=wb[name][rc * 128:(rc + 1) * 128, c0:c0 + nc_], in_=b_[:, 0:nc_]),
                    r=[("st2b", i)], w=[(("wb", name, rc, cb), half)], dma="p2b%d" % i)
                yield

    def pump(n):
        g = p2_state["gen"]
        if g is None:
            return
        for _ in range(n):
            try:
                next(g)
            except StopIteration:
                p2_state["gen"] = None
                return

    def p2_release():
        keys = [("st2f", i) for i in range(5)] + [("st2b", i) for i in range(5)]
        for eng in ("act", "dve"):
            sc.op(eng, None, w=keys)

    def attn_head(t, nr, h, ktiles, obuf, obk, ssq_, fillers=None, pv_delay=2):
        qk_ = kfm(OFF_M1, t, nr)
        a0 = ACCt[0][:, 0:129]
        a1 = ACCt[1][:, 0:129]
        acc3 = ACCall[:, :].rearrange("p (m c) -> p m c", m=2)
        ak = ("acc", 0)
        nkt = len(ktiles)
        chunks = [list(range(i, min(i + 4, nkt))) for i in range(0, nkt, 4)]
        for ci, ch in enumerate(chunks):
            if MM.i % 2 == 1:
                MM.next()
            psA, pak = MM.next()
            psB, pbk = MM.next()
            pair = (MM.i - 2) % NMM
            ps2 = MMall[:, pair * 512:pair * 512 + 1024].rearrange("p (m c) -> p m c", m=2)
            rkeys = list(qk_)
            for i in ch:
                rkeys += ktiles[i]["rk"]

            def fqk(e, ch=ch, psA=psA, psB=psB):
                ins = None
                for m, ps in ((0, psA), (1, psB)):
                    for j, i in enumerate(ch):
                        kt = ktiles[i]
                        ins = e.matmul(ps[:kt["nk"], j * nr:(j + 1) * nr], lhsT=kt["kT"](m),
                                       rhs=qT[64 * m:64 * m + 64, h, t * nr:(t + 1) * nr], start=True, stop=True)
                return ins
            sc.op("pe", fqk, r=rkeys, w=[pak, pbk])
            flush()
            pt2_, ptk = PT.next()
            pt2 = pt2_.rearrange("p (m c) -> p m c", m=2)
            ptA, ptB = pt2_[:, 0:512], pt2_[:, 512:1024]
            ptak = ptbk = ptk
            nfar = 0
            while nfar < len(ch) and ktiles[ch[nfar]]["bias"] == "far":
                nfar += 1
            if nfar:
                sc.op("act", lambda e, ps2=ps2, pt2=pt2, nfar=nfar: e.activation(
                    out=pt2[:, :, 0:nfar * nr], in_=ps2[:, :, 0:nfar * nr], func=AF.Exp, bias=cfar[:, h:h + 1]),
                    r=[pak, pbk, ("cfar", h)], w=[(ptk, jj_) for jj_ in range(nfar)])
            j = nfar
            while j < len(ch):
                kt = ktiles[ch[j]]
                nk = kt["nk"]
                bo = 0 if kt["bias"] == "sub" else 128
                w_ = 1
                if (kt["bias"] == "sub" and j + 1 < len(ch) and ktiles[ch[j + 1]]["bias"] == "diag"
                        and ktiles[ch[j + 1]]["nk"] == nk and nr == 128):
                    w_ = 2
                tmp, tmpk = PA.next()
                tmp3 = tmp[:, 0:2 * w_ * nr].rearrange("p (m c) -> p m c", m=2)
                sc.op("dve", lambda e, ps2=ps2, tmp3=tmp3, j=j, nk=nk, bo=bo, w_=w_: e.tensor_tensor(
                    out=tmp3[:nk], in0=ps2[:nk, :, j * nr:(j + w_) * nr],
                    in1=biasT[:nk, h, bo:bo + w_ * nr].unsqueeze(1).to_broadcast([nk, 2, w_ * nr]), op=ALU.add),
                    r=[pak, pbk, "biasT_d", "biasT_s"] + ([(ptk, 0)] if nfar else []), w=[tmpk])
                sc.op("act", lambda e, pt2=pt2, tmp3=tmp3, j=j, nk=nk, w_=w_: e.activation(
                    out=pt2[:nk, :, j * nr:(j + w_) * nr], in_=tmp3[:nk], func=AF.Exp),
                    r=[tmpk], w=[(ptk, jj_) for jj_ in range(j, j + w_)])
                j += w_
            last = (ci == len(chunks) - 1)

            def pv(ch=ch, ptA=ptA, ptB=ptB, ptak=ptak, ptbk=ptbk, nfar=nfar, last=last):
                rk = [(ptak, j) for j in range(len(ch))]
                for i in ch:
                    rk += ktiles[i]["vk"]
                rk.append("Vones")

                def fpv(e):
                    ins = None
                    for m, pt, a in ((0, ptA, a0), (1, ptB, a1)):
                        for j, i in enumerate(ch):
                            kt = ktiles[i]
                            nk = kt["nk"]
                            ins = e.matmul(a[:nr, :], lhsT=pt[:nk, j * nr:(j + 1) * nr], rhs=kt["v"],
                                           start=(i == 0), stop=(i == nkt - 1))
                    return ins
                sc.op("pe", fpv, r=rk, w=[ak])
                if last:
                    st, stk = PS.next()
                    sc.op("dve", lambda e: e.reciprocal(out=st[:nr, 0:2], in_=acc3[:nr, :, 128]), r=[ak], w=[(stk, 0)])
                    sc.op("dve", lambda e: e.tensor_tensor(out=st[:nr, 2:3], in0=st[:nr, 1:2], in1=neg_lam[:nr, :], op=ALU.mult),
                          r=[(stk, 0), "neg_lam"], w=[(stk, 2)])
                    oh_ = obuf[:nr, h * 128:(h + 1) * 128]
                    sc.op("act", lambda e: e.activation(out=oh_, in_=a0[:nr, 0:128], func=AF.Copy, scale=st[:nr, 0:1]),
                          r=[ak, (stk, 0)], w=[(obk, h)])
                    sc.op("dve", lambda e: e.scalar_tensor_tensor(out=oh_, in0=a1[:nr, 0:128], scalar=st[:nr, 2:3], in1=oh_,
                                                                  op0=ALU.mult, op1=ALU.add),
                          r=[ak, (stk, 2), (obk, h)], w=[(obk, h)])
            defer(pv, pv_delay)
            if fillers:
                if "all" in fillers:
                    if ci == 0:
                        for f_ in fillers["all"]:
                            f_()
                else:
                    key_ = ci if ci < len(chunks) - 2 else ci - len(chunks)
                    for f_ in fillers.get(key_, []):
                        f_()

    def attn_finish(t, nr, obuf, obk, ssq_):
        def fin():
            st, stk = ssq_
            rs, rsk = PS.next()
            ok = [(obk, h) for h in range(H)]
            sq, sqk = PA2.next()
            sc.op("act", lambda e: e.activation(out=sq[:nr, :], in_=obuf[:nr, :], func=AF.Square), r=ok, w=[sqk])
            sc.op("dve", lambda e: e.tensor_reduce(out=st[:nr, 0:8], in_=sq[:nr, :].rearrange("p (h d) -> p h d", h=8),
                                                   axis=AX.X, op=ALU.add), r=[sqk], w=[stk])
            import os
            if os.environ.get("DBG") == "obuf" and dbg_y[0] is not None:
                ydst = dbg_y[0][t]["y"]
                sc.op("pool", lambda e: e.dma_start(out=ydst, in_=obuf[:nr, :]), r=ok, w=[], dma="dbg")
            rstd_ops((st, stk), (rs, rsk), HD, 8, nr)
            on, onk = PC.next()
            sc.op("dve", lambda e: e.tensor_tensor(out=on[:nr, :].rearrange("p (h d) -> p h d", h=8),
                                                   in0=obuf[:nr, :].rearrange("p (h d) -> p h d", h=8),
                                                   in1=rs[:nr, 0:8].unsqueeze(2).to_broadcast([nr, 8, 128]), op=ALU.mult),
                  r=ok + [rsk], w=[onk])
            if os.environ.get("DBG") == "on" and dbg_y[0] is not None:
                ydst2 = dbg_y[0][t]["y"]
                sc.op("pool", lambda e: e.dma_start(out=ydst2, in_=on[:nr, :]), r=[onk], w=[], dma="dbg")
            defer(lambda: transposes(on, [onk], nr, 8, lambda c0, n: oT[:, c0:c0 + n, t * nr:(t + 1) * nr], kfm(OFF_M2, t, nr)), 5)
        defer(fin, 2)

    xpre = {}

    def macro(tiles, mode, seq=None, qi0=0, next_tiles=None):
        nt = len(tiles)
        nr = tiles[0]["nr"]
        dbg_y[0] = tiles
        ntok = nt * nr
        ss4, ss4k = PS.next()
        rs4, rs4k = PS.next()
        xts = []
        pre_ = xpre.pop(id(tiles), None)
        for t, tl in enumerate(tiles):
            if pre_ is not None:
                xt, xk = pre_[t]
            else:
                xt, xk = PB.next() if t < 3 else PA2.next()
                sc.op("sp", lambda e, xt=xt, tl=tl: e.dma_start(out=xt[:nr, :], in_=tl["x"]), w=[xk], dma="xld%d" % t)
            xts.append((xt, xk))
            sc.op("act", lambda e, xt=xt, t=t: e.activation(out=junk[:nr, :], in_=xt[:nr, :], func=AF.Square,
                                                            accum_out=ss4[:nr, t:t + 1]), r=[xk], w=[(ss4k, t), "junk"])
        rstd_ops((ss4, ss4k), (rs4, rs4k), D, nt, nr)
        for t, tl in enumerate(tiles):
            xt, xk = xts[t]
            xn, xnk = PC.next()
            sc.op("dve", lambda e, xt=xt, xn=xn, t=t: e.tensor_scalar(out=xn[:nr, :], in0=xt[:nr, :], scalar1=rs4[:nr, t:t + 1],
                                                                      scalar2=None, op0=ALU.mult), r=[xk, rs4k], w=[xnk])
            transposes(xn, [xnk], nr, 8, lambda c0, n, t=t: hT[:, c0:c0 + n, t * nr:(t + 1) * nr], kfm(OFF_M0, t, nr))
        stage('P0')

        def hT_l(t):
            return lambda kc: hT[:, kc, t * nr:(t + 1) * nr]

        def wr(slot, ncols=512, c0=0):
            return lambda kc: slot[:, kc, c0:c0 + ncols]

        import os
        cgs = [int(c) for c in os.environ.get("P1_CGS", "2,0,4,3,1,5").split(",")]
        for cg in cgs:
            slot, sk = load_w("w_in", 0, 8, cg * 512, 512)
            for t, tl in enumerate(tiles):
                ps, psk = MM.next()
                dense_mm(ps, nr, hT_l(t), 8, wr(slot), 512, kfm(OFF_M0, t, nr) + [sk], psk)
                if cg < 4:
                    isq = cg < 2
                    sq, sqk = PA.next()
                    st, stk = PS.next()
                    rs, rsk = PS.next()
                    sc.op("act", lambda e, ps=ps, sq=sq: e.activation(out=sq[:nr, :], in_=ps[:nr, :], func=AF.Square),
                          r=[psk], w=[sqk])
                    sc.op("dve", lambda e, sq=sq, st=st: e.tensor_reduce(
                        out=st[:nr, 0:8], in_=sq[:nr, :].rearrange("p (a d) -> p a d", d=64), axis=AX.X, op=ALU.add),
                        r=[sqk], w=[stk])
                    rstd_ops((st, stk), (rs, rsk), DK, 8, nr)
                    qb, qbk = PC.next()
                    if isq:
                        sc.op("dve", lambda e, ps=ps, qb=qb, rs=rs: e.tensor_tensor(
                            out=qb[:nr, 0:512].rearrange("p (a d) -> p a d", d=64), in0=ps[:nr, :].rearrange("p (a d) -> p a d", d=64),
                            in1=rs[:nr, 0:8].unsqueeze(2).to_broadcast([nr, 8, 64]), op=ALU.mult), r=[psk, rsk], w=[qbk])

                        def dstq(c0, n, t=t, cg=cg):
                            return qT[:, cg * 4 + c0:cg * 4 + c0 + n, t * nr:(t + 1) * nr]
                        defer(lambda qb=qb, qbk=qbk, dstq=dstq, t=t: transposes(
                            qb, [qbk], nr, 4, dstq, kfm(OFF_M1, t, nr), scale=gqcol), 3)
                    else:
                        hh = cg - 2
                        sc.op("dve", lambda e, ps=ps, sq=sq, rs=rs: e.tensor_tensor(
                            out=sq[:nr, :].rearrange("p (a d) -> p a d", d=64), in0=ps[:nr, :].rearrange("p (a d) -> p a d", d=64),
                            in1=rs[:nr, 0:8].unsqueeze(2).to_broadcast([nr, 8, 64]), op=ALU.mult), r=[psk, rsk], w=[sqk])
                        sc.op("pool", lambda e, sq=sq: e.tensor_tensor(
                            out=sq[:nr, :].rearrange("p (a d) -> p a d", d=64), in0=sq[:nr, :].rearrange("p (a d) -> p a d", d=64),
                            in1=gkb[:nr, :].unsqueeze(1).to_broadcast([nr, 8, 64]), op=ALU.mult), r=[sqk, "gkb"], w=[sqk])
                        sc.op("pool", lambda e, sq=sq, tl=tl, hh=hh: e.dma_start(out=tl["k"][:, hh * 512:(hh + 1) * 512], in_=sq[:nr, :]),
                              r=[sqk], w=[], dma="pad%d" % sqk[1])
                        sc.op("act", lambda e, sq=sq, qb=qb: e.activation(out=qb[:nr, 0:512], in_=sq[:nr, :], func=AF.Copy), r=[sqk], w=[qbk])
                        if mode == "prompt":
                            kt_i = qi0 + t

                            def dstk(c0, n, kt_i=kt_i, hh=hh):
                                return KT[:, hh * 4 + c0:hh * 4 + c0 + n, kt_i * 128:(kt_i + 1) * 128]
                            wkeys = [("KT", kt_i, hh)]
                        else:
                            def dstk(c0, n, t=t, hh=hh):
                                return KTn[:, hh * 4 + c0:hh * 4 + c0 + n, t * 64:(t + 1) * 64]
                            wkeys = [("KTn", t, hh)]
                        defer(lambda qb=qb, qbk=qbk, dstk=dstk, wkeys=wkeys: transposes(qb, [qbk], nr, 4, dstk, wkeys, force_act=True), 3)
                else:
                    hh = cg - 4
                    vf, vfk = PA.next()
                    sc.op("act", lambda e, ps=ps, vf=vf: e.activation(out=vf[:nr, :], in_=ps[:nr, :], func=AF.Copy), r=[psk], w=[vfk])
                    sc.op("pool", lambda e, vf=vf, tl=tl, hh=hh: e.dma_start(out=tl["v"][:, hh * 512:(hh + 1) * 512], in_=vf[:nr, :]),
                          r=[vfk], w=[], dma="pad%d" % vfk[1])
                    if mode == "prompt":
                        kt_i = qi0 + t
                        dst = Vr[:, kt_i, hh * 4:hh * 4 + 4, 0:128]
                        wk = [("V", kt_i, hh)]
                    else:
                        dst = Vn[0:64, t, hh * 4:hh * 4 + 4, 0:128]
                        wk = [("Vn", t, hh)]
                    sc.op("dve", lambda e, vf=vf, dst=dst: e.tensor_copy(out=dst[:nr], in_=vf[:nr, :].rearrange("p (h d) -> p h d", h=4)),
                          r=[vfk], w=wk)
        flush_all()
        stage('P1')
        if mode == "prompt":
            for t in range(nt):
                qi = qi0 + t
                obuf, obk = PB.next()
                ssq_ = PS.next()
                for h in range(H):
                    hh = h // 4
                    kts = []
                    for kt_i in range(qi + 1):
                        bias = "far" if kt_i <= qi - 2 else ("sub" if kt_i == qi - 1 else "diag")
                        kts.append({"kT": (lambda m, kt_i=kt_i, h=h: KT[64 * m:64 * m + 64, h, kt_i * 128:(kt_i + 1) * 128]),
                                    "v": Vr[:, kt_i, h, :], "nk": 128, "bias": bias,
                                    "rk": [("KT", kt_i, hh)], "vk": [("V", kt_i, hh)]})
                    attn_head(t, nr, h, kts, obuf, obk, ssq_)
                    pump(4)
                attn_finish(t, nr, obuf, obk, ssq_)
        else:
            def unit_load(s, h):
                ui = (s * H + h) % 2
                kst, vst, ktu, vu = units[ui]
                uk = ("unit", ui)
                srck = ck[s].rearrange("(j p) (h c) -> p j h c", p=128, h=8)[:, :, h, :]
                srcv = cv[s].rearrange("(j p) (h c) -> p j h c", p=128, h=8)[:, :, h, :]
                sc.op("sp", lambda e: e.dma_start(out=kst, in_=srck), w=[(uk, "kst")], dma="uk%d" % ui)
                sc.op("sp", lambda e: e.dma_start(out=vst, in_=srcv), w=[(uk, "vst")], dma="uv%d" % ui)
                sc.op("dve", lambda e: e.tensor_copy(out=vu[:, :, 0:128], in_=vst), r=[(uk, "vst")], w=[(uk, "vu")])

            def unit_tr(s, h):
                ui = (s * H + h) % 2
                kst, vst, ktu, vu = units[ui]
                uk = ("unit", ui)
                outl = []
                for j0 in range(0, NJ, 4):
                    def g(j0=j0):
                        n = min(4, NJ - j0)
                        ps, psk = MM.next()

                        def ftr(e):
                            ins = None
                            for j in range(n):
                                ins = e.transpose(out=ps[:, j * 128:(j + 1) * 128], in_=kst[:, j0 + j, :], identity=identf[:, :])
                            return ins
                        sc.op("pe", ftr, r=[(uk, "kst"), "identf"], w=[psk])
                        evac(ktu[:, j0 * 128:(j0 + n) * 128], ps[:, 0:n * 128], r=[psk], w=[(uk, "ktu", j0 // 4)])
                    outl.append(g)
                return outl

            U = [(s_, h_) for s_ in range(nt) for h_ in range(H)]
            unit_load(*U[0])
            for g in unit_tr(*U[0]):
                g()
            for t in range(nt):
                s = t
                obuf, obk = PB.next()
                ssq_ = PS.next()
                for h in range(H):
                    hh = h // 4
                    ui = (s * H + h) % 2
                    if NJ < 4:
                        flush_all()
                    kst, vst, ktu, vu = units[ui]
                    uk = ("unit", ui)
                    idx = s * H + h
                    fillers = None
                    if idx + 1 < len(U):
                        nx = U[idx + 1]
                        trs = unit_tr(*nx)
                        half_ = (len(trs) + 1) // 2
                        fillers = {0: [lambda nx=nx: unit_load(*nx)], -2: trs[:half_], -1: trs[half_:]}
                        nchunks_ = (NJ + 1 + 3) // 4
                        if nchunks_ < 3:
                            fillers = {"all": [lambda nx=nx: unit_load(*nx)] + trs}
                    kts = []
                    for j in range(NJ):
                        bias = "far" if j < NJ - 1 else "sub"
                        kts.append({"kT": (lambda m, j=j, ktu=ktu: ktu[64 * m:64 * m + 64, j * 128:(j + 1) * 128]),
                                    "v": vu[:, j, :], "nk": 128, "bias": bias,
                                    "rk": [(uk, "ktu", j // 4)], "vk": [(uk, "vu"), ("vuones", ui)]})
                    kts.append({"kT": (lambda m, h=h, s=s: KTn[64 * m:64 * m + 64, h, s * 64:(s + 1) * 64]),
                                "v": Vn[0:64, s, h, :], "nk": 64, "bias": "diag",
                                "rk": [("KTn", s, hh)], "vk": [("Vn", s, hh), "Vnones"]})
                    attn_head(t, nr, h, kts, obuf, obk, ssq_, fillers=fillers, pv_delay=1)
                attn_finish(t, nr, obuf, obk, ssq_)
        def p4_tile(t):
            gmb, gmbk = PC.next()
            for half in range(2):
                ps, psk = MM.next()

                def fz(e, ps=ps, half=half, t=t):
                    ins = None
                    for j in range(4):
                        g = half * 4 + j
                        ins = e.matmul(ps[:nr, j * 128:(j + 1) * 128], lhsT=wmT[:nr, g, :nr], rhs=gvb[:nr, t, g * 128:(g + 1) * 128],
                                       start=True, stop=True)
                    return ins
                sc.op("pe", fz, r=ktm(OFF_M4, t) + ["wmT"], w=[psk])
                flush()
                tmp, tmpk = PA.next()
                sc.op("dve", lambda e, ps=ps, tmp=tmp, half=half: e.tensor_tensor(
                    out=tmp[:nr, :].rearrange("p (g c) -> p g c", g=4), in0=ps[:nr, :].rearrange("p (g c) -> p g c", g=4),
                    in1=bcol[:nr, half * 4:half * 4 + 4].unsqueeze(2).to_broadcast([nr, 4, 128]), op=ALU.add),
                    r=[psk, "bcol"], w=[tmpk])
                sc.op("dve", lambda e, tmp=tmp, gmb=gmb, half=half, t=t: e.tensor_tensor(
                    out=gmb[:nr, half * 512:(half + 1) * 512], in0=tmp[:nr, :], in1=gu[:nr, t, half * 512:(half + 1) * 512], op=ALU.mult),
                    r=[tmpk] + ktm(OFF_M3, t), w=[(gmbk, half)])
            import os
            if os.environ.get("DBG") == "gmb":
                sc.op("pool", lambda e, gmb=gmb, t=t: e.dma_start(out=tiles[t]["y"], in_=gmb[:nr, :]), r=[(gmbk, 0), (gmbk, 1)], w=[], dma="dbg")
            defer(lambda gmb=gmb, gmbk=gmbk, t=t: transposes(
                gmb, [(gmbk, 0), (gmbk, 1)], nr, 8, lambda c0, n: gmT[:, c0:c0 + n, t * nr:(t + 1) * nr], kfm(OFF_M1, t, nr)), 3)

        for cg in (6, 7):
            slot, sk = load_w("w_in", 0, 8, cg * 512, 512)
            for t in range(nt):
                ps, psk = MM.next()
                dense_mm(ps, nr, hT_l(t), 8, wr(slot), 512, kfm(OFF_M0, t, nr) + [sk], psk)
                sc.op("act", lambda e, ps=ps, t=t, cg=cg: e.activation(out=gu[:nr, t, (cg - 6) * 512:(cg - 5) * 512], in_=ps[:nr, :],
                                                                       func=AF.Gelu_apprx_tanh), r=[psk], w=ktm(OFF_M3, t))
                pump(3)
        slot8, sk8 = load_w("w_in", 0, 8, 8 * 512, 512)
        slot9, sk9 = load_w("w_in", 0, 8, 9 * 512, 512)
        for t, tl in enumerate(tiles):
            gf, gfk = PB.next()
            for c2, (slot, sk) in enumerate(((slot8, sk8), (slot9, sk9))):
                ps, psk = MM.next()
                dense_mm(ps, nr, hT_l(t), 8, wr(slot), 512, kfm(OFF_M0, t, nr) + [sk], psk)
                sc.op("act", lambda e, ps=ps, gf=gf, c2=c2: e.activation(out=gf[:nr, c2 * 512:(c2 + 1) * 512], in_=ps[:nr, :],
                                                                         func=AF.Gelu_apprx_tanh), r=[psk], w=[(gfk, c2)])
            ss, ssk = PS.next()
            rs, rsk = PS.next()
            sc.op("act", lambda e, gf=gf, ss=ss: e.activation(out=junk[:nr, :], in_=gf[:nr, :], func=AF.Square, accum_out=ss[:nr, 0:1]),
                  r=[(gfk, 0), (gfk, 1)], w=[ssk, "junk"])
            rstd_ops((ss, ssk), (rs, rsk), D, 1, nr)
            if mode == "prompt":
                sc.op("dve", lambda e, gf=gf, rs=rs, t=t: e.scalar_tensor_tensor(out=gvb[:nr, t, :], in0=gf[:nr, :], scalar=rs[:nr, 0:1],
                                                                               in1=Gm[:nr, :], op0=ALU.mult, op1=ALU.mult),
                      r=[(gfk, 0), (gfk, 1), rsk, "Gm"], w=ktm(OFF_M4, t))
            else:
                sc.op("dve", lambda e, gf=gf, rs=rs: e.scalar_tensor_tensor(out=gf[:nr, :], in0=gf[:nr, :], scalar=rs[:nr, 0:1],
                                                                           in1=Gm[:nr, :], op0=ALU.mult, op1=ALU.mult),
                      r=[(gfk, 0), (gfk, 1), rsk, "Gm"], w=[(gfk, 0), (gfk, 1)])
                sc.op("pool", lambda e, gf=gf, tl=tl: e.dma_start(out=tl["gm"], in_=gf[:nr, :]), r=[(gfk, 0), (gfk, 1)], w=[],
                      dma="pbd%d" % gfk[1])
                sc.op("pool", lambda e, gf=gf, t=t: e.tensor_copy(out=gvb[:nr, t, :], in_=gf[:nr, :]), r=[(gfk, 0), (gfk, 1)],
                      w=ktm(OFF_M4, t))
            if t >= 2:
                p4_tile(t - 2)
        for t_ in range(max(0, nt - 2), nt):
            p4_tile(t_)
        stage('P3')
        flush_all()
        pump(100000)
        stage('P4')
        for c in range(2):
            s_ga, k_ga = load_w("w_in", 0, 8, (10 + c) * 512, 512)
            s_ab, k_ab = load_w("w_ab", 0, 8, c * 512, 512)
            s_gg, k_gg = load_w("w_in", 0, 8, (12 + c) * 512, 512)
            s_gb, k_gb = load_w("w_gb", 0, 8, c * 512, 512)
            for t in range(nt):
                t1, t1k = PA.next()
                t2, t2k = PA.next()
                ps, psk = MM.next()
                dense_mm(ps, nr, hT_l(t), 8, wr(s_ga), 512, kfm(OFF_M0, t, nr) + [k_ga], psk)
                sc.op("act", lambda e, ps=ps, t1=t1: e.activation(out=t1[:nr, :], in_=ps[:nr, :], func=AF.Sigmoid), r=[psk], w=[t1k])
                ps, psk = MM.next()
                dense_mm(ps, nr, lambda kc, t=t: oT[:, kc, t * nr:(t + 1) * nr], 8, wr(s_ab), 512, kfm(OFF_M2, t, nr) + [k_ab], psk)
                sc.op("dve", lambda e, ps=ps, t1=t1: e.tensor_tensor(out=t1[:nr, :], in0=ps[:nr, :], in1=t1[:nr, :], op=ALU.mult),
                      r=[psk, t1k], w=[t1k])
                ps, psk = MM.next()
                dense_mm(ps, nr, hT_l(t), 8, wr(s_gg), 512, kfm(OFF_M0, t, nr) + [k_gg], psk)
                sc.op("act", lambda e, ps=ps, t2=t2: e.activation(out=t2[:nr, :], in_=ps[:nr, :], func=AF.Sigmoid), r=[psk], w=[t2k])
                ps, psk = MM.next()
                dense_mm(ps, nr, lambda kc, t=t: gmT[:, kc, t * nr:(t + 1) * nr], 8, wr(s_gb), 512, kfm(OFF_M1, t, nr) + [k_gb], psk)
                sc.op("dve", lambda e, ps=ps, t2=t2: e.tensor_tensor(out=t2[:nr, :], in0=ps[:nr, :], in1=t2[:nr, :], op=ALU.mult),
                      r=[psk, t2k], w=[t2k])
                yb, ybk = PC.next()
                sc.op("dve", lambda e, t1=t1, t2=t2, yb=yb: e.tensor_tensor(out=yb[:nr, 0:512], in0=t1[:nr, :], in1=t2[:nr, :], op=ALU.add),
                      r=[t1k, t2k], w=[ybk])
                defer(lambda yb=yb, ybk=ybk, t=t, c=c: transposes(
                    yb, [ybk], nr, 4, lambda c0, n: yT[:, c * 4 + c0:c * 4 + c0 + n, t * nr:(t + 1) * nr], kfm(OFF_M5, t, nr)), 3)
        flush_all()
        stage('P5')
        for c in range(2):
            s_o, k_o = load_w("w_out", 0, 8, c * 512, 512)
            for t, tl in enumerate(tiles):
                if c == 0:
                    sc.op("sp", lambda e, t=t, tl=tl: e.dma_start(out=x1[:nr, t, :], in_=tl["x"]), w=kx1(t), dma="x1l%d" % t)
                ps, psk = MM.next()
                dense_mm(ps, nr, lambda kc, t=t: yT[:, kc, t * nr:(t + 1) * nr], 8, wr(s_o), 512, kfm(OFF_M5, t, nr) + [k_o], psk)
                sc.op("dve", lambda e, ps=ps, t=t, c=c: e.tensor_tensor(out=x1[:nr, t, c * 512:(c + 1) * 512], in0=ps[:nr, :],
                                                                        in1=x1[:nr, t, c * 512:(c + 1) * 512], op=ALU.add),
                      r=[psk] + kx1(t), w=kx1(t))
                if c == 1:
                    if t == 0:
                        ss6, ss6k = PS.next()
                        rs6, rs6k = PS.next()
                    sc.op("act", lambda e, t=t, ss6=ss6: e.activation(out=junk[:nr, :], in_=x1[:nr, t, :], func=AF.Square,
                                                                      accum_out=ss6[:nr, t:t + 1]), r=kx1(t), w=[(ss6k, t), "junk"])
        flush_all()
        rstd_ops((ss6, ss6k), (rs6, rs6k), D, nt, nr)
        for t in range(nt):
            xn, xnk = PC.next()
            sc.op("dve", lambda e, t=t, xn=xn: e.tensor_scalar(out=xn[:nr, :], in0=x1[:nr, t, :], scalar1=rs6[:nr, t:t + 1],
                                                               scalar2=None, op0=ALU.mult), r=kx1(t) + [rs6k], w=[xnk])
            transposes(xn, [xnk], nr, 8, lambda c0, n, t=t: hnT[:, c0:c0 + n, t * nr:(t + 1) * nr], kfm(OFF_M0, t, nr))
        stage('P6')
        if next_tiles is not None:
            lst = []
            nr2 = next_tiles[0]["nr"]
            for t2, tl2 in enumerate(next_tiles):
                xt2, xk2 = PB.next() if t2 < 3 else PA2.next()
                sc.op("sp", lambda e, xt2=xt2, tl2=tl2, nr2=nr2: e.dma_start(out=xt2[:nr2, :], in_=tl2["x"]), w=[xk2], dma="xld%d" % t2)
                lst.append((xt2, xk2))
            xpre[id(next_tiles)] = lst
        hk_all = [k for t in range(nt) for k in kfm(OFF_M0, t, nr)]
        for jg in range(0, NFC, 4):
            nj = min(4, NFC - jg)
            s_g, k_g = load_w("w_f1", 0, 8, jg * 128, nj * 128)
            s_u, k_u = load_w("w_f1", 0, 8, DFF + jg * 128, nj * 128)
            for jj in range(nj):
                j = jg + jj
                psg, psgk = MM.next()
                psu, psuk = MM.next()

                def ff(e, psg=psg, psu=psu, jj=jj, s_g=s_g, s_u=s_u):
                    ins = None
                    for ps, sl in ((psg, s_g), (psu, s_u)):
                        for kc in range(8):
                            ins = e.matmul(ps[:, 0:ntok], lhsT=sl[:, kc, jj * 128:(jj + 1) * 128], rhs=hnT[:, kc, 0:ntok],
                                           start=(kc == 0), stop=(kc == 7))
                    return ins
                sc.op("pe", ff, r=hk_all + [k_g, k_u], w=[psgk, psuk])
                flush()
                sg, sgk = PA.next()
                sc.op("act", lambda e, psg=psg, sg=sg: e.activation(out=sg[:, 0:ntok], in_=psg[:, 0:ntok], func=AF.Silu), r=[psgk], w=[sgk])
                sc.op("dve", lambda e, psu=psu, sg=sg, j=j: e.tensor_tensor(out=actT[:, j, 0:ntok], in0=psu[:, 0:ntok], in1=sg[:, 0:ntok],
                                                                          op=ALU.mult), r=[psuk, sgk], w=kact(j))
        stage('P7')
        ak_all = [k for j in range(NFC) for k in kact(j)]
        for c in range(2):
            sl = [load_w("w_f2", r0, min(8, NFC - r0), c * 512, 512) for r0 in range(0, NFC, 8)]
            pss = [MM.next() for _ in range(nt)]
            for si, (slot_, slk) in enumerate(sl):
                j0 = si * 8
                nj_ = min(8, NFC - j0)
                for t in range(nt):
                    ps, psk = pss[t]

                    def f2(e, ps=ps, t=t, slot_=slot_, j0=j0, nj_=nj_):
                        ins = None
                        for jj in range(nj_):
                            j = j0 + jj
                            ins = e.matmul(ps[:nr, :], lhsT=actT[:, j, t * nr:(t + 1) * nr], rhs=slot_[:, jj, :],
                                           start=(j == 0), stop=(j == NFC - 1))
                        return ins
                    sc.op("pe", f2, r=[k for j in range(j0, j0 + nj_) for k in kact(j)] + [slk], w=[psk])
                    flush()
            for t, tl in enumerate(tiles):
                ps, psk = pss[t]
                sc.op("dve", lambda e, ps=ps, t=t, c=c: e.tensor_tensor(out=x1[:nr, t, c * 512:(c + 1) * 512], in0=ps[:nr, :],
                                                                        in1=x1[:nr, t, c * 512:(c + 1) * 512], op=ALU.add),
                      r=[psk] + kx1(t), w=kx1(t))
                import os
                if c == 1 and not os.environ.get("DBG"):
                    sc.op("pool", lambda e, t=t, tl=tl: e.dma_start(out=tl["y"], in_=x1[:nr, t, :]), r=kx1(t), w=[], dma="x1s%d" % t)

    _stage[0] = 0
    try:
        setup()
        stage('setup')
        prepass()
        stage('prepass')
        plist = []
        for p in range(NP):
            for mt in range(S // 512):
                tiles = []
                for t in range(4):
                    r0 = p * S + mt * 512 + t * 128
                    tiles.append({"nr": 128, "x": xp[r0:r0 + 128, :], "y": yp[r0:r0 + 128, :], "k": kp[r0:r0 + 128, :],
                                  "v": vp[r0:r0 + 128, :]})
                plist.append((tiles, p, mt))
        stiles = []
        for s_ in range(NS):
            r0 = s_ * TS
            stiles.append({"nr": TS, "x": xs[r0:r0 + TS, :], "y": ys[r0:r0 + TS, :], "k": ksn[r0:r0 + TS, :],
                           "v": vsn[r0:r0 + TS, :], "gm": gms[r0:r0 + TS, :]})
        for i_, (tiles, p, mt) in enumerate(plist):
            if OVERLAP and p == 0 and mt == 0:
                p2_state["gen"] = part2_gen()
            if OVERLAP and p == 0 and mt == 2:
                p2_release()
            nxt_ = plist[i_ + 1][0] if i_ + 1 < len(plist) else (stiles if NS else None)
            macro(tiles, "prompt", seq=p, qi0=mt * 4, next_tiles=nxt_)
        if NS:
            sc.fence()
            for ui in range(2):
                vu = units[ui][3]
                sc.op("pool", lambda e, vu=vu: e.memset(vu[:, :, 128:129], 1.0), w=[("vuones", ui)])
            sc.op("pool", lambda e: e.memset(Vn[:, :, :, 128:129], 1.0), w=["Vnones"])
            tiles = stiles
            macro(tiles, "sample")

    except _Stop:
        deferred.clear()
    flush_all()
    sc.fence()
    sc.emit(nc)
    return nc


_CONST = {}


def _consts():
    if not _CONST:
        i = np.arange(LB)
        b = _bucket(127 - i)
        oh = np.zeros((32, LB), np.float32)
        oh[b, i] = 1.0
        _CONST["c_oh"] = oh
        _CONST["c_ident"] = np.eye(128, dtype=np.float32)
        _CONST["c_tril"] = np.tril(np.ones((128, 128), np.float32))
    return _CONST


_NC_CACHE = {}


def run_cfg(inputs, NP, S, NS, PAST, ncores):
    key = (NP, S, NS, PAST)
    if key not in _NC_CACHE:
        _NC_CACHE[key] = build(NP, S, NS, PAST)
    nc = _NC_CACHE[key]
    f = lambda a: np.ascontiguousarray(np.asarray(a, dtype=np.float32))
    shared = {
        "rel_table": f(inputs["rel_table"]), "norm1_g": f(inputs["norm1_g"]).reshape(1, D),
        "w_in": f(inputs["w_in"]).reshape(D, INW), "q_norm_g": f(inputs["q_norm_g"]).reshape(1, DK),
        "k_norm_g": f(inputs["k_norm_g"]).reshape(1, DK), "lambda_q1": f(inputs["lambda_q1"]).reshape(1, DK),
        "lambda_k1": f(inputs["lambda_k1"]).reshape(1, DK), "lambda_q2": f(inputs["lambda_q2"]).reshape(1, DK),
        "lambda_k2": f(inputs["lambda_k2"]).reshape(1, DK), "subln_g": f(inputs["subln_g"]).reshape(1, HD),
        "gm_norm_g": f(inputs["gm_norm_g"]).reshape(1, D), "gm_w_s": f(inputs["gm_w_s"]).reshape(8, 128, 128),
        "gm_b": f(inputs["gm_b"]).reshape(8, 128), "w_ab": f(inputs["w_attn_branch"]).reshape(D, D),
        "w_gb": f(inputs["w_gmlp_branch"]).reshape(D, D), "w_out": f(inputs["w_out"]).reshape(D, D),
        "norm2_g": f(inputs["norm2_g"]).reshape(1, D), "w_f1": f(inputs["w_ffn_in"]).reshape(D, 2 * DFF),
        "w_f2": f(inputs["w_ffn_out"]).reshape(DFF, D),
    }
    shared.update(_consts())
    xp_ = f(inputs["x_prompt"])
    xs_ = f(inputs["x_sample"])
    ck_ = f(inputs["cache_k"])[0]
    cv_ = f(inputs["cache_v"])[0]
    in_maps = []
    for c in range(ncores):
        m = dict(shared)
        m["xp"] = xp_[c * NP:(c + 1) * NP].reshape(NP * S, D)
        m["xs"] = xs_[c * NS:(c + 1) * NS].reshape(NS * 64, D)
        m["ck"] = ck_[c * NS:(c + 1) * NS].reshape(NS, PAST, D)
        m["cv"] = cv_[c * NS:(c + 1) * NS].reshape(NS, PAST, D)
        in_maps.append(m)
    res = run_bass_kernel_spmd(nc, in_maps, core_ids=list(range(ncores)))
    R = res.results
    cat = lambda n: np.concatenate([r[n] for r in R], axis=0)
    B, Bs = NP * ncores, NS * ncores
    y_p = cat("yp").reshape(B, S, D)
    y_s = cat("ys").reshape(Bs, 64, D)
    k_p = cat("kp").reshape(1, B, S, H, 2, DK)
    v_p = cat("vp").reshape(1, B, S, H, HD)
    k_s = cat("ksn").reshape(1, Bs, 64, H, 2, DK)
    v_s = cat("vsn").reshape(1, Bs, 64, H, HD)
    g_s = cat("gms").reshape(1, Bs, 64, D)
    return (y_p, y_s, k_p, v_p, k_s, v_s, g_s)


def kernel(**inputs):
    return run_cfg(inputs, 2, 2048, 4, 2048, NCORES)
```
